# Optimizing a Trainium2 kernel written in Bass

```python
import jax, jax.numpy as jnp
from jax import lax
import numpy as np

D_MODEL = 1024
BATCH = 16
SEQ = 2048
DEPTH = 2

D_FF = ((8 * D_MODEL // 3 + 255) // 256) * 256
D_CONV = D_MODEL // 2
CONV_WIDTH = 31
D_SHORT = D_MODEL // 4
SHORT_WIDTH = 3
D_POOL = D_MODEL // 4
POOL_WINDOWS = (2, 4, 8, 16)
POOL_GROUPS = len(POOL_WINDOWS)
POOL_GROUP_DIM = D_POOL // POOL_GROUPS
N_BRANCH = 3
IN_COLS = 2 * D_CONV + 3 * D_SHORT + D_POOL + N_BRANCH * D_MODEL
EPS = 1e-6

kernel_name = "hybrid_gated_conv_shortconv_pool_macaron"

_SPLITS = list(np.cumsum([D_CONV, D_CONV, D_SHORT, D_SHORT, D_SHORT, D_POOL]).tolist())


def rms_norm(x, g):
    xf = x.astype(jnp.float32)
    y = xf * lax.rsqrt(jnp.mean(xf * xf, axis=-1, keepdims=True) + EPS)
    return (y * g.astype(jnp.float32)).astype(x.dtype)


def layer_norm(x, g, b):
    xf = x.astype(jnp.float32)
    mu = jnp.mean(xf, axis=-1, keepdims=True)
    xc = xf - mu
    y = xc * lax.rsqrt(jnp.mean(xc * xc, axis=-1, keepdims=True) + EPS)
    return (y * g.astype(jnp.float32) + b.astype(jnp.float32)).astype(x.dtype)


def swiglu(h, w_gate, w_up, w_down):
    return (jax.nn.silu(h @ w_gate) * (h @ w_up)) @ w_down


def causal_depthwise_conv(u, w):
    k_width, channels = w.shape
    return lax.conv_general_dilated(
        u, w[:, None, :].astype(u.dtype),
        window_strides=(1,), padding=[(k_width - 1, 0)],
        dimension_numbers=("NWC", "WIO", "NWC"),
        feature_group_count=channels)


def multiscale_causal_pool(u):
    seq = u.shape[1]
    uf = u.astype(jnp.float32)
    cs = jnp.cumsum(uf, axis=1)
    count_base = jnp.arange(1, seq + 1, dtype=jnp.float32)[None, :, None]
    means = []
    for g, w in enumerate(POOL_WINDOWS):
        csg = cs[..., g * POOL_GROUP_DIM:(g + 1) * POOL_GROUP_DIM]
        lagged = jnp.pad(csg, ((0, 0), (w, 0), (0, 0)))[:, :seq]
        means.append((csg - lagged) / jnp.minimum(count_base, float(w)))
    return (jnp.concatenate(means, axis=-1) - uf).astype(u.dtype)


def hybrid_mixer(h, w_in, conv_dw, conv_b, conv_ln_g, conv_ln_b, w_pa,
                 short_dw, w_pb, pool_w, pool_scale, w_pc, w_o):
    bsz, seq, _ = h.shape
    u = h @ w_in
    a_val, a_gate, b_gate, c_gate, b_x, p_in, gate_logits = jnp.split(u, _SPLITS, axis=-1)

    a = a_val * jax.nn.sigmoid(a_gate)
    a = causal_depthwise_conv(a, conv_dw) + conv_b
    a = jax.nn.silu(layer_norm(a, conv_ln_g, conv_ln_b))
    y_a = a @ w_pa

    s = causal_depthwise_conv(c_gate * b_x, short_dw)
    y_b = (b_gate * s) @ w_pb

    p = multiscale_causal_pool(p_in).reshape(bsz, seq, POOL_GROUPS, POOL_GROUP_DIM)
    p = jnp.einsum("bsgc,gcd->bsgd", p, pool_w).reshape(bsz, seq, D_POOL) * pool_scale
    y_c = p @ w_pc

    g = jax.nn.sigmoid(gate_logits).reshape(bsz, seq, N_BRANCH, D_MODEL)
    merged = g[..., 0, :] * y_a + g[..., 1, :] * y_b + g[..., 2, :] * y_c
    return merged @ w_o


def setup_inputs(seed: int = 0) -> dict:
    key = jax.random.key(seed)
    ks = jax.random.split(key, 24)

    def dense(k, shape):
        return jax.random.normal(k, shape, jnp.float32) * (shape[-2] ** -0.5)

    def gain(k, shape, s=0.05):
        return 1.0 + s * jax.random.normal(k, shape, jnp.float32)

    def small(k, shape, s=0.02):
        return s * jax.random.normal(k, shape, jnp.float32)

    return {
        "x": jax.random.normal(ks[0], (BATCH, SEQ, D_MODEL), jnp.float32),
        "norm_ffn1_g": gain(ks[1], (DEPTH, D_MODEL)),
        "ffn1_w_gate": dense(ks[2], (DEPTH, D_MODEL, D_FF)),
        "ffn1_w_up": dense(ks[3], (DEPTH, D_MODEL, D_FF)),
        "ffn1_w_down": dense(ks[4], (DEPTH, D_FF, D_MODEL)),
        "norm_mix_g": gain(ks[5], (DEPTH, D_MODEL)),
        "w_in": dense(ks[6], (DEPTH, D_MODEL, IN_COLS)),
        "conv_dw": jax.random.normal(ks[7], (DEPTH, CONV_WIDTH, D_CONV), jnp.float32) * (CONV_WIDTH ** -0.5),
        "conv_b": small(ks[8], (DEPTH, D_CONV)),
        "conv_ln_g": gain(ks[9], (DEPTH, D_CONV)),
        "conv_ln_b": small(ks[10], (DEPTH, D_CONV)),
        "w_pa": dense(ks[11], (DEPTH, D_CONV, D_MODEL)),
        "short_dw": jax.random.normal(ks[12], (DEPTH, SHORT_WIDTH, D_SHORT), jnp.float32) * (SHORT_WIDTH ** -0.5),
        "w_pb": dense(ks[13], (DEPTH, D_SHORT, D_MODEL)),
        "pool_w": dense(ks[14], (DEPTH, POOL_GROUPS, POOL_GROUP_DIM, POOL_GROUP_DIM)),
        "pool_scale": gain(ks[15], (DEPTH, D_POOL), 0.1),
        "w_pc": dense(ks[16], (DEPTH, D_POOL, D_MODEL)),
        "w_o": dense(ks[17], (DEPTH, D_MODEL, D_MODEL)),
        "norm_ffn2_g": gain(ks[18], (DEPTH, D_MODEL)),
        "ffn2_w_gate": dense(ks[19], (DEPTH, D_MODEL, D_FF)),
        "ffn2_w_up": dense(ks[20], (DEPTH, D_MODEL, D_FF)),
        "ffn2_w_down": dense(ks[21], (DEPTH, D_FF, D_MODEL)),
        "final_norm_g": gain(ks[22], (D_MODEL,)),
    }


def reference(x, norm_ffn1_g, ffn1_w_gate, ffn1_w_up, ffn1_w_down, norm_mix_g, w_in,
              conv_dw, conv_b, conv_ln_g, conv_ln_b, w_pa, short_dw, w_pb, pool_w,
              pool_scale, w_pc, w_o, norm_ffn2_g, ffn2_w_gate, ffn2_w_up, ffn2_w_down,
              final_norm_g):
    for l in range(DEPTH):
        x = x + 0.5 * swiglu(rms_norm(x, norm_ffn1_g[l]), ffn1_w_gate[l], ffn1_w_up[l], ffn1_w_down[l])
        x = x + hybrid_mixer(rms_norm(x, norm_mix_g[l]), w_in[l], conv_dw[l], conv_b[l],
                             conv_ln_g[l], conv_ln_b[l], w_pa[l], short_dw[l], w_pb[l],
                             pool_w[l], pool_scale[l], w_pc[l], w_o[l])
        x = x + 0.5 * swiglu(rms_norm(x, norm_ffn2_g[l]), ffn2_w_gate[l], ffn2_w_up[l], ffn2_w_down[l])
    return rms_norm(x, final_norm_g)
```

```python
import numpy as np
from contextlib import ExitStack
import concourse.bass as bass
import concourse.mybir as mybir
from concourse.bass_utils import run_bass_kernel_spmd

F32 = mybir.dt.float32
F32R = mybir.dt.float32r
BF16 = mybir.dt.bfloat16
ALU = mybir.AluOpType
AF = mybir.ActivationFunctionType

D_MODEL = 1024
BATCH = 16
SEQ = 2048
DEPTH = 2
D_FF = 2816
FC = D_FF // 128
KC = D_MODEL // 128
EPS = 1e-6
N_CORES = 8
TT = 512
NT = 2
T = TT * NT
HALVES = SEQ // T
FFN_GROUPS = [list(range(0, 11)), list(range(11, 22))]
NS = 8
UCOLS = 2048
CW = 31
HA = CW - 1
HU = 16
HB = 2
PL = 168
NP = DEPTH * PL + 8
P_N1, P_NM, P_N2, P_CW, P_CB, P_LG, P_LB, P_SW, P_PS = 0, 8, 16, 24, 148, 152, 156, 160, 166
NCONST = 128 + 32
W_IN_ORDER = [10, 12, 8, 11, 13, 9, 14, 15, 4, 0, 5, 1, 6, 2, 7, 3]


class Op:
    __slots__ = ("eng", "fn", "deps", "idx", "key", "milestone", "mval", "clock", "dma_sem", "dma_val",
                 "waits", "has_dep", "gidx")


class Sched:
    COMPUTE = ("pe", "act", "dve", "pool")

    def __init__(self):
        self.q = {e: [] for e in ("pe", "act", "dve", "pool", "sp")}
        self.all = []
        self.last_w = {}
        self.readers = {}
        self.region_users = {}
        self.dma_counts = {}

    def op(self, eng, fn, reads=(), writes=(), regions=(), dma_sem=None):
        o = Op()
        o.eng = eng
        o.fn = fn
        o.milestone = False
        o.has_dep = False
        o.dma_sem = dma_sem
        deps = set()
        for r in reads:
            w = self.last_w.get(r)
            if w is not None:
                deps.add(w)
        for w_ in writes:
            lw = self.last_w.get(w_)
            if lw is not None:
                deps.add(lw)
            rl = self.readers.get(w_)
            if rl:
                deps.update(rl)
        for (reg, tenant) in regions:
            ru = self.region_users.setdefault(reg, {})
            for t2, d in ru.items():
                if t2 != tenant:
                    deps.update(d.values())
            if dma_sem is None:
                ru.setdefault(tenant, {})[eng] = o
            else:
                ru.setdefault(tenant, {})[("dma", id(o))] = o
        for r in reads:
            self.readers.setdefault(r, []).append(o)
        for w_ in writes:
            self.last_w[w_] = o
            self.readers[w_] = []
        o.deps = deps
        for d in deps:
            d.has_dep = True
        o.idx = len(self.q[eng])
        self.q[eng].append(o)
        if dma_sem is not None:
            n = self.dma_counts.get(dma_sem, 0) + 1
            self.dma_counts[dma_sem] = n
            o.key = ("dma", dma_sem)
            o.dma_val = n
        else:
            o.key = eng
        o.gidx = len(self.all)
        self.all.append(o)
        return o

    def finalize(self):
        known = {e: {} for e in self.q}
        for o in self.all:
            kn = known[o.eng]
            waits = []
            for d in sorted(o.deps, key=lambda z: -z.gidx):
                if d.dma_sem is not None:
                    k, v = d.key, d.dma_val
                else:
                    k, v = d.key, d.idx
                    if d.eng == o.eng and o.eng == "pe":
                        continue
                if kn.get(k, -1) >= v:
                    continue
                waits.append(d)
                d.milestone = True
                for k2, v2 in d.clock.items():
                    if kn.get(k2, -1) < v2:
                        kn[k2] = v2
            best = {}
            for d in waits:
                v = d.dma_val if d.dma_sem is not None else d.idx
                if d.key not in best or v > best[d.key][0]:
                    best[d.key] = (v, d)
            o.waits = [b[1] for b in best.values()]
            if o.has_dep:
                c = dict(kn)
                if o.dma_sem is not None:
                    c[o.key] = o.dma_val
                else:
                    c[o.key] = o.idx
                o.clock = c
            else:
                o.clock = None
        for o in self.q["pe"]:
            o.milestone = True
        for e in self.COMPUTE:
            n = 0
            for o in self.q[e]:
                if o.dma_sem is None and o.milestone:
                    n += 1
                    o.mval = n
        pos = {e: 0 for e in self.q}
        done = set()
        progress = True
        total = len(self.all)
        ndone = 0
        while progress:
            progress = False
            for e, ql in self.q.items():
                while pos[e] < len(ql):
                    o = ql[pos[e]]
                    if all(id(d) in done for d in o.waits):
                        done.add(id(o))
                        pos[e] += 1
                        ndone += 1
                        progress = True
                    else:
                        break
        if ndone != total:
            raise RuntimeError("static deadlock in schedule: %s" % {e: (pos[e], len(self.q[e])) for e in self.q})


def _kxn_block(w, cidx):
    k = w.shape[0] // 128
    blk = w[:, cidx * 128:(cidx + 1) * 128].reshape(k, 128, 128)
    return np.ascontiguousarray(blk.transpose(1, 0, 2)).reshape(128, k * 128)


ALL_STAGES = ("ffn1", "mix", "ffn2")


def build_unit_list(depth, stages=ALL_STAGES):
    units = []
    for l in range(depth):
        for which in (1, 2):
            if ("ffn%d" % which) not in stages:
                if which == 1 and "mix" in stages:
                    pass
                else:
                    continue
            for g, fs in enumerate(FFN_GROUPS):
                if ("ffn%d" % which) not in stages:
                    break
                for f in fs:
                    units.append(("gu", l, which, f))
                for i in range(0, len(fs), 2):
                    units.append(("wd", l, which, tuple(fs[i:i + 2])))
            if which == 1 and "mix" in stages:
                for i in range(0, 16, 2):
                    units.append(("win", l, W_IN_ORDER[i], W_IN_ORDER[i + 1]))
                units.append(("poolw", l))
                for j in range(KC):
                    units.append(("p2a", l, j))
                    units.append(("p2b", l, j))
                for i in range(0, KC, 2):
                    units.append(("wo", l, i))
    return units


def build_wstream(inp, depth, stages=ALL_STAGES):
    units = build_unit_list(depth, stages)
    ws = np.zeros((len(units), 128, UCOLS), np.float32)
    for n, u in enumerate(units):
        kind, l = u[0], u[1]
        if kind == "gu":
            which, f = u[2], u[3]
            wg = inp["ffn1_w_gate"] if which == 1 else inp["ffn2_w_gate"]
            wu = inp["ffn1_w_up"] if which == 1 else inp["ffn2_w_up"]
            ws[n, :, 0:1024] = _kxn_block(wg[l], f)
            ws[n, :, 1024:2048] = _kxn_block(wu[l], f)
        elif kind == "wd":
            which, fs = u[2], u[3]
            wd = (inp["ffn1_w_down"] if which == 1 else inp["ffn2_w_down"])[l]
            for i, f in enumerate(fs):
                ws[n, :, i * 1024:(i + 1) * 1024] = wd[f * 128:(f + 1) * 128, :]
        elif kind == "win":
            ws[n, :, 0:1024] = _kxn_block(inp["w_in"][l], u[2])
            ws[n, :, 1024:2048] = _kxn_block(inp["w_in"][l], u[3])
        elif kind == "poolw":
            pw = inp["pool_w"][l]
            for c in range(2):
                for gg in range(2):
                    g = 2 * c + gg
                    ws[n, gg * 64:(gg + 1) * 64, c * 128 + gg * 64: c * 128 + (gg + 1) * 64] = pw[g]
        elif kind == "p2a":
            j = u[2]
            ws[n, :, 0:1024] = _kxn_block(inp["w_in"][l], 16 + j)
            ws[n, :, 1024:2048] = _kxn_block(inp["w_in"][l], 24 + j)
        elif kind == "p2b":
            j = u[2]
            ws[n, :, 0:1024] = _kxn_block(inp["w_in"][l], 32 + j)
            ws[n, :, 1024:1536] = _kxn_block(inp["w_pa"][l], j)
            ws[n, :, 1536:1792] = _kxn_block(inp["w_pb"][l], j)
            ws[n, :, 1792:2048] = _kxn_block(inp["w_pc"][l], j)
        elif kind == "wo":
            i = u[2]
            ws[n, :, 0:1024] = _kxn_block(inp["w_o"][l], i)
            ws[n, :, 1024:2048] = _kxn_block(inp["w_o"][l], i + 1)
    return ws


def build_params(inp, depth):
    p = np.zeros((128, NP), np.float32)

    def pc(v):
        return np.ascontiguousarray(np.asarray(v, np.float32).reshape(-1, 128).T)
    for l in range(depth):
        b = l * PL
        p[:, b + P_N1:b + P_N1 + 8] = pc(inp["norm_ffn1_g"][l])
        p[:, b + P_NM:b + P_NM + 8] = pc(inp["norm_mix_g"][l])
        p[:, b + P_N2:b + P_N2 + 8] = pc(inp["norm_ffn2_g"][l])
        cw = np.asarray(inp["conv_dw"][l], np.float32)
        p[:, b + P_CW:b + P_CW + 124] = cw.reshape(CW, 4, 128).transpose(2, 1, 0).reshape(128, 124)
        p[:, b + P_CB:b + P_CB + 4] = pc(inp["conv_b"][l])
        p[:, b + P_LG:b + P_LG + 4] = pc(inp["conv_ln_g"][l])
        p[:, b + P_LB:b + P_LB + 4] = pc(inp["conv_ln_b"][l])
        sw = np.asarray(inp["short_dw"][l], np.float32)
        p[:, b + P_SW:b + P_SW + 6] = sw.reshape(3, 2, 128).transpose(2, 1, 0).reshape(128, 6)
        p[:, b + P_PS:b + P_PS + 2] = pc(inp["pool_scale"][l])
    p[:, depth * PL:depth * PL + 8] = pc(inp["final_norm_g"])
    return p


def build_consts():
    c = np.zeros((128, NCONST), np.float32)
    c[:, 0:128] = np.eye(128, dtype=np.float32)
    wins = {(0, 0): 2, (0, 1): 4, (1, 0): 8, (1, 1): 16}
    for ch in range(2):
        for hf in range(2):
            w = wins[(ch, hf)]
            for t in range(16):
                c[hf * 64:(hf + 1) * 64, 128 + ch * 16 + t] = 1.0 / min(t + 1, w)
    return c


def build_program(n_tiles, depth, stages=ALL_STAGES, debug=False):
    nc = bass.Bass("TRN2", target_bir_lowering=False)
    unit_list = build_unit_list(depth, stages)
    NU = len(unit_list)
    xT = nc.dram_tensor("xT", [n_tiles, 128, KC, T], F32, kind="ExternalInput").ap()
    wstream = nc.dram_tensor("wstream", [max(NU, 1), 128, UCOLS], F32, kind="ExternalInput").ap()
    params_d = nc.dram_tensor("params", [128, NP], F32, kind="ExternalInput").ap()
    consts_d = nc.dram_tensor("consts", [128, NCONST], F32, kind="ExternalInput").ap()
    yT = nc.dram_tensor("yT", [n_tiles, 128, KC, T], F32, kind="ExternalOutput").ap()
    if debug:
        dbg = {
            "a": nc.dram_tensor("dbg_a", [128, 4, T + HA], F32, kind="ExternalOutput").ap(),
            "co": nc.dram_tensor("dbg_co", [128, 4, T], F32, kind="ExternalOutput").ap(),
            "aact": nc.dram_tensor("dbg_aact", [128, 4, T], BF16, kind="ExternalOutput").ap(),
            "sbb": nc.dram_tensor("dbg_sbb", [128, 2, T], BF16, kind="ExternalOutput").ap(),
            "pp": nc.dram_tensor("dbg_pp", [128, 2, T], BF16, kind="ExternalOutput").ap(),
            "mg": nc.dram_tensor("dbg_mg", [128, KC, T], BF16, kind="ExternalOutput").ap(),
            "h": nc.dram_tensor("dbg_h", [128, KC, T], BF16, kind="ExternalOutput").ap(),
        }

    S = Sched()
    es = ExitStack()
    with es:
        def sb(name, shape, dt):
            return es.enter_context(nc.sbuf_tensor(name, shape, dt))
        x = sb("x", [128, KC, T], F32)
        h = sb("h", [128, KC, T], BF16)
        ring = sb("ring", [128, NS, UCOLS], BF16)
        sq = sb("sq", [128, KC, TT], BF16)
        ones_bf = sb("ones_bf", [128, 128], BF16)
        tga = sb("tga", [128, 3, TT], F32)
        rs = sb("rs", [128, NT, TT], F32)
        lnm = sb("lnm", [128, NT, TT], F32)
        lnv = sb("lnv", [128, NT, TT], F32)
        lnq = sb("lnq", [128, TT], F32)
        params = sb("params_sb", [128, NP], F32)
        consts = sb("consts_sb", [128, NCONST], F32)
        ones = sb("ones", [128, 128], F32)
        epsc = sb("epsc", [128, 1], F32)
        R1 = sb("R1", [128, 4 * (T + HA)], F32)
        R2 = sb("R2", [128, 4 * T], F32)
        R3 = sb("R3", [128, 7168], F32)
        a_act = sb("a_act", [128, 4, T], BF16)
        diag = sb("diag", [128, CW, 128], F32)
        sbb = sb("sbb", [128, 2, T], BF16)
        pp = sb("pp", [128, 2, T], BF16)
        halo_a = sb("halo_a", [128, depth, 4, HA], F32)
        halo_cb = sb("halo_cb", [128, depth, 2, HB], F32)
        halo_u = sb("halo_u", [128, depth, 2, HU], F32)
        ps = es.enter_context(nc.psum_tensor("ps", [128, 8, TT], F32))

        a_v = R1[:, :].rearrange("p (c t) -> p c t", c=4)
        mg_v = R3[:, 3072:7168].bitcast(BF16).rearrange("p (c t) -> p c t", c=KC)
        co_v = R2[:, :].rearrange("p (c t) -> p c t", c=4)
        act_v = R3[:, 0:5632].bitcast(BF16).rearrange("p (f t) -> p f t", f=11)
        sg_v = R3[:, 5632:5632 + 3 * TT].rearrange("p (r t) -> p r t", r=3)
        cg_v = R3[:, 0:2 * TT].rearrange("p (r t) -> p r t", r=2)
        cb_v = R3[:, 1024:1024 + 2 * (T + HB)].rearrange("p (c t) -> p c t", c=2)
        s_v = R3[:, 3200:3200 + 2 * TT].rearrange("p (r t) -> p r t", r=2)
        u_v = R3[:, 0:2 * (T + HU)].rearrange("p (c t) -> p c t", c=2)
        r1_v = R3[:, 2080:2080 + (T + HU)]
        r2_v = R3[:, 3120:3120 + (T + HU)]
        c16_v = R3[:, 4160:4176]
        p_v = R3[:, 4224:4224 + T].bitcast(BF16).rearrange("p (c t) -> p c t", c=2)
        tg_v = R3[:, 0:3 * 2 * TT].rearrange("p (i r t) -> p i r t", i=3, r=2)
        HKEYS = [("h", c_, t_) for c_ in range(KC) for t_ in range(NT)]
        stage_parts = [
            ("yo_a", a_act, "a_act", 0, 4),
            ("yo_b", sbb, "sbb", 4, 2),
            ("yo_c", pp, "pp", 6, 2),
        ]

        def stage_chunk(c):
            for (_, buf, nm, c0, n) in stage_parts:
                if c0 <= c < c0 + n:
                    return buf[:, c - c0, :].bitcast(F32), [(nm, c - c0, t_) for t_ in range(NT)]

        ident_v = consts[:, 0:128]
        corr_v = consts[:, 128:160].rearrange("p (c t) -> p c t", c=2)

        state = {"bank": 0, "sq": 0, "sg": 0, "tg": 0, "tga": 0}

        def new_bank():
            b = state["bank"]
            state["bank"] = (b + 1) % 8
            return b

        def rot(name, n):
            v = state[name]
            state[name] = (v + 1) % n
            return v

        total_units = n_tiles * NU
        wstate = {"issued": 0}

        def issue_unit():
            n = wstate["issued"]
            if n >= total_units:
                return
            wstate["issued"] = n + 1
            slot = n % NS
            u = n % NU
            S.op("pool", lambda e, slot=slot, u=u: e.dma_start(out=ring[:, slot, :], in_=wstream[u]),
                 writes=[("ring", slot)], dma_sem="ring%d" % slot)

        cursor = {"n": 0}

        def next_unit(expect_kind):
            n = cursor["n"]
            cursor["n"] = n + 1
            assert unit_list[n % NU][0] == expect_kind, (unit_list[n % NU], expect_kind)
            return n % NS

        def release_unit():
            issue_unit()

        S.op("sp", lambda e: e.dma_start(out=params[:, :], in_=params_d), writes=["params"], dma_sem="params")
        S.op("sp", lambda e: e.dma_start(out=consts[:, :], in_=consts_d), writes=["consts"], dma_sem="consts")
        S.op("dve", lambda e: e.memset(ones[:, :], 1.0), writes=["ones"])
        S.op("dve", lambda e: e.memset(ones_bf[:, :], 1.0), writes=["ones_bf"])
        S.op("dve", lambda e: e.memset(epsc[:, :], EPS), writes=["epsc"])
        for _ in range(NS):
            issue_unit()

        def load_x_tt(tile, tt):
            S.op("sp", lambda e, tile=tile, tt=tt: e.dma_start(
                out=x[:, :, tt * TT:(tt + 1) * TT], in_=xT[tile, :, :, tt * TT:(tt + 1) * TT]),
                writes=[("x", c, tt) for c in range(KC)], dma_sem="x%d" % tt)

        def norm_sq(tt):
            for c in range(KC):
                S.op("act", lambda e, c=c, tt=tt: e.activation(
                    out=sq[:, c, :], in_=x[:, c, tt * TT:(tt + 1) * TT], func=AF.Square),
                    reads=[("x", c, tt)], writes=[("sq", c)])

        def norm_fin(tt):
            b = new_bank()
            for c in range(KC):
                S.op("pe", lambda e, c=c, b=b: e.matmul(
                    ps[:, b, :], ones_bf[:, :], sq[:, c, :], start=(c == 0), stop=(c == KC - 1)),
                    reads=[("sq", c), "ones_bf"], writes=[("ps", b)])
            S.op("act", lambda e, b=b, tt=tt: e.activation(
                out=rs[:, tt, :], in_=ps[:, b, :], func=AF.Ln, bias=epsc[:, 0:1], scale=1.0 / D_MODEL),
                reads=[("ps", b), "epsc"], writes=[("rs", tt)])
            S.op("act", lambda e, tt=tt: e.activation(
                out=rs[:, tt, :], in_=rs[:, tt, :], func=AF.Exp, scale=-0.5),
                reads=[("rs", tt)], writes=[("rs", tt)])

        def norm_stats(tt):
            norm_sq(tt)
            norm_fin(tt)

        def norm_apply(tt, gcol):
            for c in range(KC):
                S.op("dve", lambda e, c=c, tt=tt, gcol=gcol: e.scalar_tensor_tensor(
                    out=h[:, c, tt * TT:(tt + 1) * TT], in0=x[:, c, tt * TT:(tt + 1) * TT],
                    scalar=params[:, gcol + c:gcol + c + 1], in1=rs[:, tt, :], op0=ALU.mult, op1=ALU.mult),
                    reads=[("x", c, tt), ("rs", tt), "params"], writes=[("h", c, tt)])

        def final_apply(tile, tt, gcol):
            for c in range(KC):
                sv, keys = stage_chunk(c)
                S.op("dve", lambda e, c=c, tt=tt, gcol=gcol, sv=sv: e.scalar_tensor_tensor(
                    out=sv, in0=x[:, c, tt * TT:(tt + 1) * TT],
                    scalar=params[:, gcol + c:gcol + c + 1], in1=rs[:, tt, :], op0=ALU.mult, op1=ALU.mult),
                    reads=[("x", c, tt), ("rs", tt), "params"], writes=keys)
            for (sem, buf, nm, c0, n) in stage_parts:
                src = buf[:, :, :].rearrange("p c t -> p (c t)").bitcast(F32).rearrange("p (c t) -> p c t", c=n)
                S.op("sp", lambda e, tile=tile, tt=tt, src=src, c0=c0, n=n: e.dma_start(
                    out=yT[tile, :, c0:c0 + n, tt * TT:(tt + 1) * TT], in_=src),
                    reads=[(nm, c_, t_) for c_ in range(n) for t_ in range(NT)], dma_sem=sem)
            if tile + 1 < n_tiles:
                load_x_tt(tile + 1, tt)

        def ffn(l, which, after_tt, first_block=0, mid_hook=None):
            for g, fs in enumerate(FFN_GROUPS):
                nfb = first_block if g == 0 else 0
                sched = []
                if nfb:
                    fslots = [next_unit("gu") for _ in range(nfb)]
                    for tt in range(NT):
                        for fi in range(nfb):
                            sched.append((fi, fslots[fi], tt, tt == NT - 1))
                for fi in range(nfb, len(fs)):
                    slot = None
                    for tt in range(NT):
                        sched.append((fi, slot, tt, tt == NT - 1))
                hook = [mid_hook if g == 0 else None]
                for (fi, slot, tt, rel) in sched:
                    if tt == 1 and hook[0] is not None:
                        hook[0]()
                        hook[0] = None
                    if slot is None:
                        if tt == 0:
                            cur_slot = next_unit("gu")
                        slot = cur_slot
                    for _once in (0,):
                        bg = new_bank()
                        for kc in range(KC):
                            S.op("pe", lambda e, bg=bg, slot=slot, kc=kc, tt=tt: e.matmul(
                                ps[:, bg, :], ring[:, slot, kc * 128:(kc + 1) * 128],
                                h[:, kc, tt * TT:(tt + 1) * TT], start=(kc == 0), stop=(kc == KC - 1)),
                                reads=[("ring", slot), ("h", kc, tt)], writes=[("ps", bg)])
                        bu = new_bank()
                        for kc in range(KC):
                            S.op("pe", lambda e, bu=bu, slot=slot, kc=kc, tt=tt: e.matmul(
                                ps[:, bu, :], ring[:, slot, 1024 + kc * 128:1024 + (kc + 1) * 128],
                                h[:, kc, tt * TT:(tt + 1) * TT], start=(kc == 0), stop=(kc == KC - 1)),
                                reads=[("ring", slot), ("h", kc, tt)], writes=[("ps", bu)])
                        k = rot("sg", 3)
                        S.op("act", lambda e, bg=bg, k=k: e.activation(
                            out=sg_v[:, k, :], in_=ps[:, bg, :], func=AF.Silu),
                            reads=[("ps", bg)], writes=[("sg", k)], regions=[("R3", "ffn")])
                        S.op("dve", lambda e, bu=bu, k=k, fi=fi, tt=tt: e.tensor_tensor(
                            out=act_v[:, fi, tt * TT:(tt + 1) * TT], in0=sg_v[:, k, :], in1=ps[:, bu, :],
                            op=ALU.mult),
                            reads=[("ps", bu), ("sg", k)], writes=[("act", fi, tt)], regions=[("R3", "ffn")])
                    if rel:
                        release_unit()
                nwd = (len(fs) + 1) // 2
                wslots = [next_unit("wd") for _ in range(nwd)]
                last = (g == len(FFN_GROUPS) - 1)
                for tt in range(NT):
                    for d in range(KC):
                        b = new_bank()
                        for fi in range(len(fs)):
                            ws_ = wslots[fi // 2]
                            off = (fi % 2) * 1024 + d * 128
                            S.op("pe", lambda e, b=b, ws_=ws_, off=off, fi=fi, tt=tt, n=len(fs): e.matmul(
                                ps[:, b, :], ring[:, ws_, off:off + 128], act_v[:, fi, tt * TT:(tt + 1) * TT],
                                start=(fi == 0), stop=(fi == n - 1)),
                                reads=[("ring", ws_), ("act", fi, tt)], writes=[("ps", b)],
                                regions=[("R3", "ffn")])
                        S.op("dve", lambda e, b=b, d=d, tt=tt: e.scalar_tensor_tensor(
                            out=x[:, d, tt * TT:(tt + 1) * TT], in0=ps[:, b, :], scalar=0.5,
                            in1=x[:, d, tt * TT:(tt + 1) * TT], op0=ALU.mult, op1=ALU.add),
                            reads=[("ps", b), ("x", d, tt)], writes=[("x", d, tt)])
                        if last and tt == 1 and d == 2:
                            after_tt.fin(0)
                    if last:
                        after_tt.sq(tt)
                if last:
                    after_tt.fin(NT - 1)
                for _ in range(nwd):
                    release_unit()

        def win_chunk_mm(slot, half_idx, tt):
            b = new_bank()
            for kc in range(KC):
                S.op("pe", lambda e, b=b, slot=slot, kc=kc, tt=tt, o=half_idx * 1024: e.matmul(
                    ps[:, b, :], ring[:, slot, o + kc * 128:o + (kc + 1) * 128],
                    h[:, kc, tt * TT:(tt + 1) * TT], start=(kc == 0), stop=(kc == KC - 1)),
                    reads=[("ring", slot), ("h", kc, tt)], writes=[("ps", b)])
            return b

        def mixer(l, half, after_tt):
            pb = l * PL
            RA = [("R1", "a")]
            RCO = [("R2", "co")]
            for j in range(4):
                if half == 0:
                    S.op("dve", lambda e, j=j: e.memset(a_v[:, j, 0:HA], 0.0),
                         writes=[("a_halo", j)], regions=RA)
                else:
                    S.op("dve", lambda e, j=j, l=l: e.tensor_copy(a_v[:, j, 0:HA].bitcast(F32R), halo_a[:, l, j, :]),
                         reads=[("halo_a", l, j)], writes=[("a_halo", j)], regions=RA)

            def av(j):
                slot = next_unit("win")
                for tt in range(NT):
                    b1 = win_chunk_mm(slot, 0, tt)
                    k = rot("tga", 3)
                    S.op("act", lambda e, b1=b1, k=k: e.activation(
                        out=tga[:, k, :], in_=ps[:, b1, :], func=AF.Tanh, scale=0.5),
                        reads=[("ps", b1)], writes=[("tga", k)])
                    b2 = win_chunk_mm(slot, 1, tt)
                    S.op("dve", lambda e, b2=b2, k=k, j=j, tt=tt: e.scalar_tensor_tensor(
                        out=a_v[:, j, HA + tt * TT:HA + (tt + 1) * TT].bitcast(F32R), in0=tga[:, k, :], scalar=1.0,
                        in1=ps[:, b2, :], op0=ALU.add, op1=ALU.mult),
                        reads=[("ps", b2), ("tga", k)], writes=[("a", j, tt)], regions=RA)
                release_unit()
                if half == 0:
                    S.op("dve", lambda e, j=j, l=l: e.tensor_copy(halo_a[:, l, j, :], a_v[:, j, T:T + HA]),
                         reads=[("a", j, NT - 1)], writes=[("halo_a", l, j)], regions=RA)

            DH = 16

            def diag_build(j):
                for (eng, k0, k1, hk) in (("dve", 0, DH, 0), ("pool", DH, CW, 1)):
                    cwv = params[:, pb + P_CW + j * CW + k0: pb + P_CW + j * CW + k1]
                    S.op(eng, lambda e, cwv=cwv, k0=k0, k1=k1: e.tensor_tensor(
                        out=diag[:, k0:k1, :].bitcast(F32R),
                        in0=ident_v.unsqueeze(1).to_broadcast([128, k1 - k0, 128]),
                        in1=cwv.unsqueeze(2).to_broadcast([128, k1 - k0, 128]), op=ALU.mult),
                        reads=["consts", "params"], writes=[("diag", hk)])

            def conv(j):
                banks = [new_bank() for _ in range(NT)]
                for k in range(CW):
                    for tt in range(NT):
                        rd = [("a", j, tt), ("diag", 0 if k < DH else 1)]
                        if tt == 0:
                            rd.append(("a_halo", j))
                        else:
                            rd.append(("a", j, tt - 1))
                        S.op("pe", lambda e, k=k, tt=tt, j=j, b=banks[tt]: e.matmul(
                            ps[:, b, :], diag[:, k, :].bitcast(F32R),
                            a_v[:, j, tt * TT + k:tt * TT + k + TT].bitcast(F32R),
                            start=(k == 0), stop=(k == CW - 1)),
                            reads=rd, writes=[("ps", banks[tt])], regions=RA)
                for tt in range(NT):
                    S.op("act", lambda e, tt=tt, j=j, b=banks[tt]: e.activation(
                        out=co_v[:, j, tt * TT:(tt + 1) * TT].bitcast(F32R), in_=ps[:, b, :], func=AF.Identity,
                        bias=params[:, pb + P_CB + j:pb + P_CB + j + 1], scale=0.5),
                        reads=[("ps", banks[tt]), "params"], writes=[("co", j, tt)], regions=RCO)

            def ln_sq(tt):
                for j in range(4):
                    k = tt * 4 + j
                    S.op("act", lambda e, j=j, k=k, tt=tt: e.activation(
                        out=sq[:, k, :], in_=co_v[:, j, tt * TT:(tt + 1) * TT], func=AF.Square),
                        reads=[("co", j, tt)], writes=[("sq", k)], regions=RCO)

            def ln_fin(tt):
                b1 = new_bank()
                b2 = new_bank()
                for j in range(4):
                    k = tt * 4 + j
                    S.op("pe", lambda e, j=j, tt=tt, b1=b1: e.matmul(
                        ps[:, b1, :], ones[:, :].bitcast(F32R), co_v[:, j, tt * TT:(tt + 1) * TT].bitcast(F32R),
                        start=(j == 0), stop=(j == 3)),
                        reads=[("co", j, tt), "ones"], writes=[("ps", b1)], regions=RCO)
                    S.op("pe", lambda e, j=j, k=k, b2=b2: e.matmul(
                        ps[:, b2, :], ones_bf[:, :], sq[:, k, :],
                        start=(j == 0), stop=(j == 3)),
                        reads=[("sq", k), "ones_bf"], writes=[("ps", b2)])
                S.op("dve", lambda e, tt=tt, b1=b1: e.tensor_scalar(
                    lnm[:, tt, :], ps[:, b1, :], 1.0 / 512.0, None, op0=ALU.mult),
                    reads=[("ps", b1)], writes=[("lnm", tt)])
                S.op("dve", lambda e, tt=tt: e.tensor_tensor(
                    out=lnq[:, :], in0=lnm[:, tt, :], in1=lnm[:, tt, :], op=ALU.mult),
                    reads=[("lnm", tt)], writes=["lnq"])
                S.op("dve", lambda e, tt=tt, b2=b2: e.scalar_tensor_tensor(
                    out=lnv[:, tt, :], in0=ps[:, b2, :], scalar=1.0 / 512.0, in1=lnq[:, :],
                    op0=ALU.mult, op1=ALU.subtract),
                    reads=[("ps", b2), "lnq"], writes=[("lnv", tt)])
                S.op("act", lambda e, tt=tt: e.activation(
                    out=lnv[:, tt, :], in_=lnv[:, tt, :], func=AF.Ln, bias=epsc[:, 0:1], scale=1.0),
                    reads=[("lnv", tt), "epsc"], writes=[("lnv", tt)])
                S.op("act", lambda e, tt=tt: e.activation(
                    out=lnv[:, tt, :], in_=lnv[:, tt, :], func=AF.Exp, scale=-0.5),
                    reads=[("lnv", tt)], writes=[("lnv", tt)])

            def ln_apply(tt):
                for j in range(4):
                    S.op("dve", lambda e, j=j, tt=tt: e.tensor_tensor(
                        out=co_v[:, j, tt * TT:(tt + 1) * TT].bitcast(F32R), in0=co_v[:, j, tt * TT:(tt + 1) * TT],
                        in1=lnm[:, tt, :], op=ALU.subtract),
                        reads=[("co", j, tt), ("lnm", tt)], writes=[("co", j, tt)], regions=RCO)
                    S.op("dve", lambda e, j=j, tt=tt: e.tensor_tensor(
                        out=co_v[:, j, tt * TT:(tt + 1) * TT].bitcast(F32R), in0=co_v[:, j, tt * TT:(tt + 1) * TT],
                        in1=lnv[:, tt, :], op=ALU.mult),
                        reads=[("co", j, tt), ("lnv", tt)], writes=[("co", j, tt)], regions=RCO)
                    S.op("act", lambda e, j=j, tt=tt: e.activation(
                        out=a_act[:, j, tt * TT:(tt + 1) * TT], in_=co_v[:, j, tt * TT:(tt + 1) * TT],
                        func=AF.Silu, bias=params[:, pb + P_LB + j:pb + P_LB + j + 1],
                        scale=params[:, pb + P_LG + j:pb + P_LG + j + 1]),
                        reads=[("co", j, tt), "params"], writes=[("a_act", j, tt)], regions=RCO)

            RB = [("R3", "B")]

            def b_cgate(c, slot, hi):
                for tt in range(NT):
                    b = win_chunk_mm(slot, hi, tt)
                    S.op("act", lambda e, b=b, c=c, tt=tt: e.activation(
                        out=cg_v[:, tt, :], in_=ps[:, b, :], func=AF.Copy),
                        reads=[("ps", b)], writes=[("cg", tt)], regions=RB)

            def b_bx(c, slot, hi):
                if half == 0:
                    S.op("dve", lambda e, c=c: e.memset(cb_v[:, c, 0:HB], 0.0),
                         writes=[("cb_halo", c)], regions=RB)
                else:
                    S.op("dve", lambda e, c=c, l=l: e.tensor_copy(cb_v[:, c, 0:HB], halo_cb[:, l, c, :]),
                         reads=[("halo_cb", l, c)], writes=[("cb_halo", c)], regions=RB)
                for tt in range(NT):
                    b = win_chunk_mm(slot, hi, tt)
                    S.op("dve", lambda e, b=b, c=c, tt=tt: e.tensor_tensor(
                        out=cb_v[:, c, HB + tt * TT:HB + (tt + 1) * TT], in0=cg_v[:, tt, :], in1=ps[:, b, :],
                        op=ALU.mult),
                        reads=[("ps", b), ("cg", tt)], writes=[("cb", c, tt)], regions=RB)
                if half == 0:
                    S.op("dve", lambda e, c=c, l=l: e.tensor_copy(halo_cb[:, l, c, :], cb_v[:, c, T:T + HB]),
                         reads=[("cb", c, NT - 1)], writes=[("halo_cb", l, c)], regions=RB)

            def b_bgate(c, slot, hi):
                for tt in range(NT):
                    rd = [("cb", c, tt), "params"] + ([("cb_halo", c)] if tt == 0 else [("cb", c, tt - 1)])
                    for k in range(3):
                        wcol = pb + P_SW + c * 3 + k
                        if k == 0:
                            S.op("dve", lambda e, c=c, tt=tt, wcol=wcol: e.tensor_scalar(
                                s_v[:, tt, :], cb_v[:, c, tt * TT:tt * TT + TT], params[:, wcol:wcol + 1], None,
                                op0=ALU.mult),
                                reads=rd, writes=[("s", tt)], regions=RB)
                        else:
                            S.op("dve", lambda e, c=c, tt=tt, k=k, wcol=wcol: e.scalar_tensor_tensor(
                                out=s_v[:, tt, :], in0=cb_v[:, c, tt * TT + k:tt * TT + k + TT],
                                scalar=params[:, wcol:wcol + 1], in1=s_v[:, tt, :], op0=ALU.mult, op1=ALU.add),
                                reads=rd + [("s", tt)], writes=[("s", tt)], regions=RB)
                    b = win_chunk_mm(slot, hi, tt)
                    S.op("dve", lambda e, b=b, c=c, tt=tt: e.tensor_tensor(
                        out=sbb[:, c, tt * TT:(tt + 1) * TT], in0=s_v[:, tt, :], in1=ps[:, b, :], op=ALU.mult),
                        reads=[("ps", b), ("s", tt)], writes=[("sbb", c, tt)], regions=RB)

            RC = [("R3", "C")]

            def c_pin(c, slot, hi):
                if half == 0:
                    S.op("dve", lambda e, c=c: e.memset(u_v[:, c, 0:HU], 0.0), writes=[("u_halo", c)], regions=RC)
                else:
                    S.op("dve", lambda e, c=c, l=l: e.tensor_copy(u_v[:, c, 0:HU], halo_u[:, l, c, :]),
                         reads=[("halo_u", l, c)], writes=[("u_halo", c)], regions=RC)
                for tt in range(NT):
                    b = win_chunk_mm(slot, hi, tt)
                    S.op("act", lambda e, b=b, c=c, tt=tt: e.activation(
                        out=u_v[:, c, HU + tt * TT:HU + (tt + 1) * TT], in_=ps[:, b, :], func=AF.Copy),
                        reads=[("ps", b)], writes=[("u", c, tt)], regions=RC)
                if half == 0:
                    S.op("dve", lambda e, c=c, l=l: e.tensor_copy(halo_u[:, l, c, :], u_v[:, c, T:T + HU]),
                         reads=[("u", c, NT - 1)], writes=[("halo_u", l, c)], regions=RC)

            def c_pool(c):
                W = T + HU
                uall = [("u", c, tt) for tt in range(NT)] + [("u_halo", c)]
                uu = u_v[:, c, :]
                S.op("dve", lambda e, uu=uu: e.tensor_tensor(
                    out=r1_v[:, 1:W], in0=uu[:, 1:W], in1=uu[:, 0:W - 1], op=ALU.add),
                    reads=uall, writes=["r1"], regions=RC)
                if c == 0:
                    S.op("dve", lambda e: e.tensor_tensor(
                        out=r2_v[64:128, 3:W], in0=r1_v[64:128, 3:W], in1=r1_v[64:128, 1:W - 2], op=ALU.add),
                        reads=["r1"], writes=["r2"], regions=RC)
                    lo, hi_, wl, wh = r1_v, r2_v, 2, 4
                else:
                    S.op("dve", lambda e: e.tensor_tensor(
                        out=r2_v[:, 3:W], in0=r1_v[:, 3:W], in1=r1_v[:, 1:W - 2], op=ALU.add),
                        reads=["r1"], writes=["r2"], regions=RC)
                    S.op("dve", lambda e: e.tensor_tensor(
                        out=r1_v[:, 7:W], in0=r2_v[:, 7:W], in1=r2_v[:, 3:W - 4], op=ALU.add),
                        reads=["r2"], writes=["r1"], regions=RC)
                    S.op("dve", lambda e: e.tensor_tensor(
                        out=r2_v[64:128, 15:W], in0=r1_v[64:128, 15:W], in1=r1_v[64:128, 7:W - 8], op=ALU.add),
                        reads=["r1"], writes=["r2"], regions=RC)
                    lo, hi_, wl, wh = r1_v, r2_v, 8, 16
                for (rr, p0, p1, w) in ((lo, 0, 64, wl), (hi_, 64, 128, wh)):
                    S.op("dve", lambda e, rr=rr, p0=p0, p1=p1, w=w, c=c: e.scalar_tensor_tensor(
                        out=p_v[p0:p1, c, :], in0=rr[p0:p1, HU:W], scalar=1.0 / w, in1=u_v[p0:p1, c, HU:W],
                        op0=ALU.mult, op1=ALU.subtract),
                        reads=["r1", "r2"] + uall, writes=[("p", c, 0), ("p", c, 1)], regions=RC)
                    if half == 0:
                        S.op("dve", lambda e, rr=rr, p0=p0, p1=p1, c=c: e.tensor_tensor(
                            out=c16_v[p0:p1, :], in0=rr[p0:p1, HU:HU + 16], in1=corr_v[p0:p1, c, :], op=ALU.mult),
                            reads=["r1", "r2", "consts"], writes=["c16"], regions=RC)
                        S.op("dve", lambda e, p0=p0, p1=p1, c=c: e.tensor_tensor(
                            out=p_v[p0:p1, c, 0:16], in0=c16_v[p0:p1, :], in1=u_v[p0:p1, c, HU:HU + 16],
                            op=ALU.subtract),
                            reads=["c16"] + uall, writes=[("p", c, 0)], regions=RC)

            def c_pool_mm(c, pslot):
                for tt in range(NT):
                    b = new_bank()
                    S.op("pe", lambda e, b=b, c=c, tt=tt, pslot=pslot: e.matmul(
                        ps[:, b, :], ring[:, pslot, c * 128:(c + 1) * 128], p_v[:, c, tt * TT:(tt + 1) * TT],
                        start=True, stop=True),
                        reads=[("ring", pslot), ("p", c, tt)], writes=[("ps", b)], regions=RC)
                    S.op("act", lambda e, b=b, c=c, tt=tt: e.activation(
                        out=pp[:, c, tt * TT:(tt + 1) * TT], in_=ps[:, b, :], func=AF.Copy,
                        scale=params[:, pb + P_PS + c:pb + P_PS + c + 1]),
                        reads=[("ps", b), "params"], writes=[("pp", c, tt)])

            s0 = next_unit("win")
            b_cgate(0, s0, 0)
            b_bx(0, s0, 1)
            release_unit()
            s1 = next_unit("win")
            b_bgate(0, s1, 0)
            b_cgate(1, s1, 1)
            release_unit()
            s2 = next_unit("win")
            b_bx(1, s2, 0)
            b_bgate(1, s2, 1)
            release_unit()
            s3 = next_unit("win")
            c_pin(0, s3, 0)
            c_pin(1, s3, 1)
            release_unit()
            diag_build(0)
            av(0)
            c_pool(0)
            av(1)
            conv(0)
            diag_build(1)
            c_pool(1)
            av(2)
            conv(1)
            diag_build(2)
            av(3)
            conv(2)
            diag_build(3)
            pslot = next_unit("poolw")
            c_pool_mm(0, pslot)
            c_pool_mm(1, pslot)
            release_unit()
            conv(3)
            for tt in range(NT):
                ln_sq(tt)
            ln_pending = [True]

            if debug and l == 0 and half == 0:
                S.op("sp", lambda e: e.dma_start(out=dbg["a"], in_=a_v[:, :, :]),
                     reads=[("a", j_, t_) for j_ in range(4) for t_ in range(NT)], regions=RA, dma_sem="dbg")
                S.op("sp", lambda e: e.dma_start(out=dbg["co"], in_=co_v[:, :, :]),
                     reads=[("co", j_, t_) for j_ in range(4) for t_ in range(NT)]
                     + [("a_act", j_, t_) for j_ in range(4) for t_ in range(NT)], regions=RCO, dma_sem="dbg")
                S.op("sp", lambda e: e.dma_start(out=dbg["aact"], in_=a_act[:, :, :]),
                     reads=[("a_act", j_, t_) for j_ in range(4) for t_ in range(NT)], dma_sem="dbg")
                S.op("sp", lambda e: e.dma_start(out=dbg["sbb"], in_=sbb[:, :, :]),
                     reads=[("sbb", j_, t_) for j_ in range(2) for t_ in range(NT)], dma_sem="dbg")
                S.op("sp", lambda e: e.dma_start(out=dbg["pp"], in_=pp[:, :, :]),
                     reads=[("pp", j_, t_) for j_ in range(2) for t_ in range(NT)], dma_sem="dbg")
                S.op("sp", lambda e: e.dma_start(out=dbg["h"], in_=h[:, :, :]),
                     reads=HKEYS, dma_sem="dbg")
            RP = [("R3", "P2")]
            RM = RP
            RT = RP
            for j in range(KC):
                sa = next_unit("p2a")
                sb_ = next_unit("p2b")
                rr = []
                for tt in range(NT):
                    r = rot("tg", 2)
                    rr.append(r)
                    gbanks = []
                    for i, (slot, off) in enumerate(((sa, 0), (sa, 1024), (sb_, 0))):
                        b = new_bank()
                        gbanks.append(b)
                        for kc in range(KC):
                            S.op("pe", lambda e, b=b, slot=slot, off=off, kc=kc, tt=tt: e.matmul(
                                ps[:, b, :], ring[:, slot, off + kc * 128:off + (kc + 1) * 128],
                                h[:, kc, tt * TT:(tt + 1) * TT], start=(kc == 0), stop=(kc == KC - 1)),
                                reads=[("ring", slot), ("h", kc, tt)], writes=[("ps", b)])
                    if ln_pending[0]:
                        ln_fin(tt)
                    for i, b in enumerate(gbanks):
                        S.op("act", lambda e, b=b, i=i, r=r: e.activation(
                            out=tg_v[:, i, r, :], in_=ps[:, b, :], func=AF.Tanh, scale=0.5),
                            reads=[("ps", b)], writes=[("tg", i, r)], regions=RT)
                    if ln_pending[0]:
                        ln_apply(tt)
                ln_pending[0] = False
                for tt in range(NT):
                    r = rr[tt]
                    ybanks = []
                    for (src, nk, off, nm) in ((a_act, 4, 1024, "a_act"), (sbb, 2, 1536, "sbb"), (pp, 2, 1792, "pp")):
                        b = new_bank()
                        for kc in range(nk):
                            S.op("pe", lambda e, b=b, src=src, kc=kc, nk=nk, off=off, tt=tt, sb_=sb_: e.matmul(
                                ps[:, b, :], ring[:, sb_, off + kc * 128:off + (kc + 1) * 128],
                                src[:, kc, tt * TT:(tt + 1) * TT], start=(kc == 0), stop=(kc == nk - 1)),
                                reads=[("ring", sb_), (nm, kc, tt)], writes=[("ps", b)])
                        ybanks.append(b)
                    for i in range(3):
                        S.op("dve", lambda e, i=i, r=r, b=ybanks[i]: e.scalar_tensor_tensor(
                            out=tg_v[:, i, r, :], in0=tg_v[:, i, r, :], scalar=1.0, in1=ps[:, b, :],
                            op0=ALU.add, op1=ALU.mult),
                            reads=[("ps", ybanks[i]), ("tg", i, r)], writes=[("tg", i, r)], regions=RP)
                    S.op("dve", lambda e, r=r: e.tensor_tensor(
                        out=tg_v[:, 0, r, :], in0=tg_v[:, 0, r, :], in1=tg_v[:, 1, r, :], op=ALU.add),
                        reads=[("tg", 0, r), ("tg", 1, r)], writes=[("tg", 0, r)], regions=RP)
                    S.op("dve", lambda e, r=r, j=j, tt=tt: e.tensor_tensor(
                        out=mg_v[:, j, tt * TT:(tt + 1) * TT], in0=tg_v[:, 0, r, :], in1=tg_v[:, 2, r, :],
                        op=ALU.add),
                        reads=[("tg", 0, r), ("tg", 2, r)], writes=[("mg", j, tt)], regions=RP)
                release_unit()
                release_unit()
            if debug and l == 0 and half == 0:
                S.op("sp", lambda e: e.dma_start(out=dbg["mg"], in_=mg_v[:, :, :]),
                     reads=[("mg", j_, t_) for j_ in range(KC) for t_ in range(NT)], regions=RP, dma_sem="dbg")
            wos = [next_unit("wo") for _ in range(4)]
            for tt in range(NT):
                for d in range(KC):
                    b = new_bank()
                    for kc in range(KC):
                        off = (d % 2) * 1024 + kc * 128
                        S.op("pe", lambda e, b=b, sl=wos[d // 2], off=off, kc=kc, tt=tt: e.matmul(
                            ps[:, b, :], ring[:, sl, off:off + 128], mg_v[:, kc, tt * TT:(tt + 1) * TT],
                            start=(kc == 0), stop=(kc == KC - 1)),
                            reads=[("ring", wos[d // 2]), ("mg", kc, tt)], writes=[("ps", b)], regions=RM)
                    S.op("dve", lambda e, b=b, d=d, tt=tt: e.scalar_tensor_tensor(
                        out=x[:, d, tt * TT:(tt + 1) * TT], in0=ps[:, b, :], scalar=0.5,
                        in1=x[:, d, tt * TT:(tt + 1) * TT], op0=ALU.mult, op1=ALU.add),
                        reads=[("ps", b), ("x", d, tt)], writes=[("x", d, tt)])
                    if tt == 1 and d == 2:
                        after_tt.fin(0)
                after_tt.sq(tt)
            after_tt.fin(NT - 1)
            for _ in range(4):
                release_unit()

        for tile in range(n_tiles):
            half = tile % HALVES
            if tile == 0:
                for tt in range(NT):
                    load_x_tt(tile, tt)
            first = [st for st in ALL_STAGES if st in stages]
            g0 = {"ffn1": P_N1, "mix": P_NM, "ffn2": P_N2}[first[0]] if first else P_N1
            norm_stats(0)
            norm_apply(0, 0 * PL + g0)

            def tile_start_tt1(g0=g0):
                norm_stats(1)
                norm_apply(1, 0 * PL + g0)
            ts_hook = [tile_start_tt1]
            for l in range(depth):
                pb = l * PL

                class After:
                    def __init__(self, fin_fn, defer0=True):
                        self.fin_fn = fin_fn
                        self.defer0 = defer0
                        self.done = set()

                    def sq(self, tt):
                        norm_sq(tt)
                        if tt == 0 and not self.defer0:
                            self.fin(0)

                    def fin(self, tt):
                        if tt in self.done:
                            return
                        self.done.add(tt)
                        norm_fin(tt)
                        self.fin_fn(tt)

                def fin_ffn1(tt, pb=pb):
                    norm_apply(tt, pb + P_NM)

                def fin_mix(tt, pb=pb):
                    norm_apply(tt, pb + P_N2)

                def fin_ffn2(tt, l=l, tile=tile, g0=g0):
                    if l + 1 < depth:
                        norm_apply(tt, (l + 1) * PL + g0)
                    else:
                        final_apply(tile, tt, depth * PL)
                after_ffn1, after_mix = After(fin_ffn1), After(fin_mix)
                after_ffn2 = After(fin_ffn2, defer0=(l + 1 < depth))

                seq = []
                if "ffn1" in stages:
                    seq.append("ffn1")
                if "mix" in stages:
                    seq.append("mix")
                if "ffn2" in stages:
                    seq.append("ffn2")
                for si, st in enumerate(seq):
                    if si + 1 < len(seq):
                        nxt = {"mix": after_ffn1, "ffn2": after_mix}[seq[si + 1]]
                    else:
                        nxt = after_ffn2
                    if l == 0 and si == 0 and st != "ffn1" and ts_hook[0] is not None:
                        ts_hook[0]()
                        ts_hook[0] = None
                    if st == "ffn1":
                        ts = (l == 0 and si == 0)
                        mh = ts_hook[0] if ts else None
                        if ts:
                            ts_hook[0] = None
                        ffn(l, 1, nxt, first_block=(6 if ts else 4), mid_hook=mh)
                    elif st == "mix":
                        mixer(l, half, nxt)
                    else:
                        ffn(l, 2, nxt, first_block=4)
                if not seq:
                    if ts_hook[0] is not None:
                        ts_hook[0]()
                        ts_hook[0] = None
                    for tt in range(NT):
                        after_ffn2.sq(tt)
                        after_ffn2.fin(tt)
        assert cursor["n"] == total_units, (cursor["n"], total_units)
        S.finalize()

        dma_names = sorted(S.dma_counts.keys())
        sems = {}
        for e_ in ("pe", "act", "dve", "pool"):
            sems[e_] = es.enter_context(nc.semaphore("s_" + e_))
        dsems = {}
        for nm in dma_names:
            dsems[nm] = es.enter_context(nc.semaphore("d_" + nm))
        block = es.enter_context(nc.Block())
        def emit_all():
            def run(engname):
                def body(e):
                    for o in S.q[engname]:
                        ws = []
                        for d in o.waits:
                            if d.dma_sem is not None:
                                ws.append((dsems[d.dma_sem], 16 * d.dma_val))
                            else:
                                ws.append((sems[d.eng], d.mval))
                        for (s_, v) in ws[1:]:
                            e.wait_ge(s_, v)
                        ins = o.fn(e)
                        if ws:
                            ins._wait_ge(ws[0][0], ws[0][1])
                        if o.dma_sem is not None:
                            ins.then_inc(dsems[o.dma_sem], 16)
                        elif o.milestone:
                            ins.then_inc(sems[engname], 1)
                    if engname == "sp":
                        for sem_ in ("yo_a", "yo_b", "yo_c"):
                            e.wait_ge(dsems[sem_], 16 * S.dma_counts[sem_])
                        if "dbg" in dsems:
                            e.wait_ge(dsems["dbg"], 16 * S.dma_counts["dbg"])
                return body
            block.tensor(run("pe"))
            block.scalar(run("act"))
            block.vector(run("dve"))
            block.gpsimd(run("pool"))
            block.sync(run("sp"))
        emit_all()
    return nc


def _to_tiles(xc):
    nseq = xc.shape[0]
    v = xc.reshape(nseq, HALVES, T, KC, 128)
    v = v.transpose(0, 1, 4, 3, 2)
    return np.ascontiguousarray(v).reshape(nseq * HALVES, 128, KC, T)


def _from_tiles(yt, nseq):
    v = yt.reshape(nseq, HALVES, 128, KC, T).transpose(0, 1, 4, 3, 2)
    return np.ascontiguousarray(v).reshape(nseq, SEQ, D_MODEL)


def kernel(**inputs):
    inp = {k: np.asarray(v) for k, v in inputs.items()}
    x = np.asarray(inp["x"], np.float32)
    seq_per_core = BATCH // N_CORES
    n_tiles = seq_per_core * HALVES
    ws = build_wstream(inp, DEPTH)
    params = build_params(inp, DEPTH)
    consts = build_consts()
    nc = build_program(n_tiles, DEPTH)
    in_maps = []
    for i in range(N_CORES):
        xt = _to_tiles(x[i * seq_per_core:(i + 1) * seq_per_core])
        in_maps.append({"xT": xt, "wstream": ws, "params": params, "consts": consts})
    res = run_bass_kernel_spmd(nc, in_maps, core_ids=list(range(N_CORES)))
    outs = [_from_tiles(np.asarray(r["yT"]), seq_per_core) for r in res.results]
    return np.concatenate(outs, axis=0).astype(np.float32)
```

```python
import numpy as np
from contextlib import ExitStack
import concourse.bass as bass
import concourse.mybir as mybir
from concourse.bass_utils import run_bass_kernel_spmd

F32 = mybir.dt.float32
F32R = mybir.dt.float32r
BF16 = mybir.dt.bfloat16
ALU = mybir.AluOpType
AF = mybir.ActivationFunctionType

D_MODEL = 1024
BATCH = 16
SEQ = 2048
DEPTH = 2
D_FF = 2816
FC = D_FF // 128
KC = D_MODEL // 128
EPS = 1e-6
N_CORES = 8
TT = 512
NT = 2
T = TT * NT
HALVES = SEQ // T
FFN_GROUPS = [list(range(0, 11)), list(range(11, 22))]
NS = 8
UCOLS = 2048
CW = 31
HA = CW - 1
HU = 16
HB = 2
PL = 168
NP = DEPTH * PL + 8
P_N1, P_NM, P_N2, P_CW, P_CB, P_LG, P_LB, P_SW, P_PS = 0, 8, 16, 24, 148, 152, 156, 160, 166
NCONST = 128 + 32
W_IN_ORDER = [10, 12, 8, 11, 13, 9, 14, 15, 4, 0, 5, 1, 6, 2, 7, 3]


class Op:
    __slots__ = ("eng", "fn", "deps", "idx", "key", "milestone", "mval", "clock", "dma_sem", "dma_val",
                 "waits", "has_dep", "gidx")


class Sched:
    COMPUTE = ("pe", "act", "dve", "pool")

    def __init__(self):
        self.q = {e: [] for e in ("pe", "act", "dve", "pool", "sp")}
        self.all = []
        self.last_w = {}
        self.readers = {}
        self.region_users = {}
        self.dma_counts = {}

    def op(self, eng, fn, reads=(), writes=(), regions=(), dma_sem=None):
        o = Op()
        o.eng = eng
        o.fn = fn
        o.milestone = False
        o.has_dep = False
        o.dma_sem = dma_sem
        deps = set()
        for r in reads:
            w = self.last_w.get(r)
            if w is not None:
                deps.add(w)
        for w_ in writes:
            lw = self.last_w.get(w_)
            if lw is not None:
                deps.add(lw)
            rl = self.readers.get(w_)
            if rl:
                deps.update(rl)
        for (reg, tenant) in regions:
            ru = self.region_users.setdefault(reg, {})
            for t2, d in ru.items():
                if t2 != tenant:
                    deps.update(d.values())
            if dma_sem is None:
                ru.setdefault(tenant, {})[eng] = o
            else:
                ru.setdefault(tenant, {})[("dma", id(o))] = o
        for r in reads:
            self.readers.setdefault(r, []).append(o)
        for w_ in writes:
            self.last_w[w_] = o
            self.readers[w_] = []
        o.deps = deps
        for d in deps:
            d.has_dep = True
        o.idx = len(self.q[eng])
        self.q[eng].append(o)
        if dma_sem is not None:
            n = self.dma_counts.get(dma_sem, 0) + 1
            self.dma_counts[dma_sem] = n
            o.key = ("dma", dma_sem)
            o.dma_val = n
        else:
            o.key = eng
        o.gidx = len(self.all)
        self.all.append(o)
        return o

    def finalize(self):
        known = {e: {} for e in self.q}
        for o in self.all:
            kn = known[o.eng]
            waits = []
            for d in sorted(o.deps, key=lambda z: -z.gidx):
                if d.dma_sem is not None:
                    k, v = d.key, d.dma_val
                else:
                    k, v = d.key, d.idx
                    if d.eng == o.eng and o.eng == "pe":
                        continue
                if kn.get(k, -1) >= v:
                    continue
                waits.append(d)
                d.milestone = True
                for k2, v2 in d.clock.items():
                    if kn.get(k2, -1) < v2:
                        kn[k2] = v2
            best = {}
            for d in waits:
                v = d.dma_val if d.dma_sem is not None else d.idx
                if d.key not in best or v > best[d.key][0]:
                    best[d.key] = (v, d)
            o.waits = [b[1] for b in best.values()]
            if o.has_dep:
                c = dict(kn)
                if o.dma_sem is not None:
                    c[o.key] = o.dma_val
                else:
                    c[o.key] = o.idx
                o.clock = c
            else:
                o.clock = None
        for o in self.q["pe"]:
            o.milestone = True
        for e in self.COMPUTE:
            n = 0
            for o in self.q[e]:
                if o.dma_sem is None and o.milestone:
                    n += 1
                    o.mval = n
        pos = {e: 0 for e in self.q}
        done = set()
        progress = True
        total = len(self.all)
        ndone = 0
        while progress:
            progress = False
            for e, ql in self.q.items():
                while pos[e] < len(ql):
                    o = ql[pos[e]]
                    if all(id(d) in done for d in o.waits):
                        done.add(id(o))
                        pos[e] += 1
                        ndone += 1
                        progress = True
                    else:
                        break
        if ndone != total:
            raise RuntimeError("static deadlock in schedule: %s" % {e: (pos[e], len(self.q[e])) for e in self.q})


def _kxn_block(w, cidx):
    k = w.shape[0] // 128
    blk = w[:, cidx * 128:(cidx + 1) * 128].reshape(k, 128, 128)
    return np.ascontiguousarray(blk.transpose(1, 0, 2)).reshape(128, k * 128)


ALL_STAGES = ("ffn1", "mix", "ffn2")


def build_unit_list(depth, stages=ALL_STAGES):
    units = []
    for l in range(depth):
        for which in (1, 2):
            if ("ffn%d" % which) not in stages:
                if which == 1 and "mix" in stages:
                    pass
                else:
                    continue
            for g, fs in enumerate(FFN_GROUPS):
                if ("ffn%d" % which) not in stages:
                    break
                for f in fs:
                    units.append(("gu", l, which, f))
                for i in range(0, len(fs), 2):
                    units.append(("wd", l, which, tuple(fs[i:i + 2])))
            if which == 1 and "mix" in stages:
                for i in range(0, 16, 2):
                    units.append(("win", l, W_IN_ORDER[i], W_IN_ORDER[i + 1]))
                units.append(("poolw", l))
                for j in range(KC):
                    units.append(("p2a", l, j))
                    units.append(("p2b", l, j))
                for i in range(0, KC, 2):
                    units.append(("wo", l, i))
    return units


def build_wstream(inp, depth, stages=ALL_STAGES):
    units = build_unit_list(depth, stages)
    ws = np.zeros((len(units), 128, UCOLS), np.float32)
    for n, u in enumerate(units):
        kind, l = u[0], u[1]
        if kind == "gu":
            which, f = u[2], u[3]
            wg = inp["ffn1_w_gate"] if which == 1 else inp["ffn2_w_gate"]
            wu = inp["ffn1_w_up"] if which == 1 else inp["ffn2_w_up"]
            ws[n, :, 0:1024] = _kxn_block(wg[l], f)
            ws[n, :, 1024:2048] = _kxn_block(wu[l], f)
        elif kind == "wd":
            which, fs = u[2], u[3]
            wd = (inp["ffn1_w_down"] if which == 1 else inp["ffn2_w_down"])[l]
            for i, f in enumerate(fs):
                ws[n, :, i * 1024:(i + 1) * 1024] = wd[f * 128:(f + 1) * 128, :]
        elif kind == "win":
            ws[n, :, 0:1024] = _kxn_block(inp["w_in"][l], u[2])
            ws[n, :, 1024:2048] = _kxn_block(inp["w_in"][l], u[3])
        elif kind == "poolw":
            pw = inp["pool_w"][l]
            for c in range(2):
                for gg in range(2):
                    g = 2 * c + gg
                    ws[n, gg * 64:(gg + 1) * 64, c * 128 + gg * 64: c * 128 + (gg + 1) * 64] = pw[g]
        elif kind == "p2a":
            j = u[2]
            ws[n, :, 0:1024] = _kxn_block(inp["w_in"][l], 16 + j)
            ws[n, :, 1024:2048] = _kxn_block(inp["w_in"][l], 24 + j)
        elif kind == "p2b":
            j = u[2]
            ws[n, :, 0:1024] = _kxn_block(inp["w_in"][l], 32 + j)
            ws[n, :, 1024:1536] = _kxn_block(inp["w_pa"][l], j)
            ws[n, :, 1536:1792] = _kxn_block(inp["w_pb"][l], j)
            ws[n, :, 1792:2048] = _kxn_block(inp["w_pc"][l], j)
        elif kind == "wo":
            i = u[2]
            ws[n, :, 0:1024] = _kxn_block(inp["w_o"][l], i)
            ws[n, :, 1024:2048] = _kxn_block(inp["w_o"][l], i + 1)
    return ws


def build_params(inp, depth):
    p = np.zeros((128, NP), np.float32)

    def pc(v):
        return np.ascontiguousarray(np.asarray(v, np.float32).reshape(-1, 128).T)
    for l in range(depth):
        b = l * PL
        p[:, b + P_N1:b + P_N1 + 8] = pc(inp["norm_ffn1_g"][l])
        p[:, b + P_NM:b + P_NM + 8] = pc(inp["norm_mix_g"][l])
        p[:, b + P_N2:b + P_N2 + 8] = pc(inp["norm_ffn2_g"][l])
        cw = np.asarray(inp["conv_dw"][l], np.float32)
        p[:, b + P_CW:b + P_CW + 124] = cw.reshape(CW, 4, 128).transpose(2, 1, 0).reshape(128, 124)
        p[:, b + P_CB:b + P_CB + 4] = pc(inp["conv_b"][l])
        p[:, b + P_LG:b + P_LG + 4] = pc(inp["conv_ln_g"][l])
        p[:, b + P_LB:b + P_LB + 4] = pc(inp["conv_ln_b"][l])
        sw = np.asarray(inp["short_dw"][l], np.float32)
        p[:, b + P_SW:b + P_SW + 6] = sw.reshape(3, 2, 128).transpose(2, 1, 0).reshape(128, 6)
        p[:, b + P_PS:b + P_PS + 2] = pc(inp["pool_scale"][l])
    p[:, depth * PL:depth * PL + 8] = pc(inp["final_norm_g"])
    return p


def build_consts():
    c = np.zeros((128, NCONST), np.float32)
    c[:, 0:128] = np.eye(128, dtype=np.float32)
    wins = {(0, 0): 2, (0, 1): 4, (1, 0): 8, (1, 1): 16}
    for ch in range(2):
        for hf in range(2):
            w = wins[(ch, hf)]
            for t in range(16):
                c[hf * 64:(hf + 1) * 64, 128 + ch * 16 + t] = 1.0 / min(t + 1, w)
    return c


def build_program(n_tiles, depth, stages=ALL_STAGES, debug=False):
    nc = bass.Bass("TRN2", target_bir_lowering=False)
    unit_list = build_unit_list(depth, stages)
    NU = len(unit_list)
    xT = nc.dram_tensor("xT", [n_tiles, 128, KC, T], F32, kind="ExternalInput").ap()
    wstream = nc.dram_tensor("wstream", [max(NU, 1), 128, UCOLS], F32, kind="ExternalInput").ap()
    params_d = nc.dram_tensor("params", [128, NP], F32, kind="ExternalInput").ap()
    consts_d = nc.dram_tensor("consts", [128, NCONST], F32, kind="ExternalInput").ap()
    yT = nc.dram_tensor("yT", [n_tiles, 128, KC, T], F32, kind="ExternalOutput").ap()
    if debug:
        dbg = {
            "a": nc.dram_tensor("dbg_a", [128, 4, T + HA], F32, kind="ExternalOutput").ap(),
            "co": nc.dram_tensor("dbg_co", [128, 4, T], F32, kind="ExternalOutput").ap(),
            "aact": nc.dram_tensor("dbg_aact", [128, 4, T], BF16, kind="ExternalOutput").ap(),
            "sbb": nc.dram_tensor("dbg_sbb", [128, 2, T], BF16, kind="ExternalOutput").ap(),
            "pp": nc.dram_tensor("dbg_pp", [128, 2, T], BF16, kind="ExternalOutput").ap(),
            "mg": nc.dram_tensor("dbg_mg", [128, KC, T], BF16, kind="ExternalOutput").ap(),
            "h": nc.dram_tensor("dbg_h", [128, KC, T], BF16, kind="ExternalOutput").ap(),
        }

    S = Sched()
    es = ExitStack()
    with es:
        def sb(name, shape, dt):
            return es.enter_context(nc.sbuf_tensor(name, shape, dt))
        x = sb("x", [128, KC, T], F32)
        h = sb("h", [128, KC, T], BF16)
        ring = sb("ring", [128, NS, UCOLS], BF16)
        sq = sb("sq", [128, KC, TT], BF16)
        ones_bf = sb("ones_bf", [128, 128], BF16)
        tga = sb("tga", [128, 3, TT], F32)
        rs = sb("rs", [128, NT, TT], F32)
        lnm = sb("lnm", [128, NT, TT], F32)
        lnv = sb("lnv", [128, NT, TT], F32)
        lnq = sb("lnq", [128, TT], F32)
        params = sb("params_sb", [128, NP], F32)
        consts = sb("consts_sb", [128, NCONST], F32)
        ones = sb("ones", [128, 128], F32)
        epsc = sb("epsc", [128, 1], F32)
        R1 = sb("R1", [128, 4 * (T + HA)], F32)
        R2 = sb("R2", [128, 4 * T], F32)
        R3 = sb("R3", [128, 7168], F32)
        a_act = sb("a_act", [128, 4, T], BF16)
        diag = sb("diag", [128, CW, 128], F32)
        sbb = sb("sbb", [128, 2, T], BF16)
        pp = sb("pp", [128, 2, T], BF16)
        halo_a = sb("halo_a", [128, depth, 4, HA], F32)
        halo_cb = sb("halo_cb", [128, depth, 2, HB], F32)
        halo_u = sb("halo_u", [128, depth, 2, HU], F32)
        ps = es.enter_context(nc.psum_tensor("ps", [128, 8, TT], F32))

        a_v = R1[:, :].rearrange("p (c t) -> p c t", c=4)
        mg_v = R3[:, 3072:7168].bitcast(BF16).rearrange("p (c t) -> p c t", c=KC)
        co_v = R2[:, :].rearrange("p (c t) -> p c t", c=4)
        act_v = R3[:, 0:5632].bitcast(BF16).rearrange("p (f t) -> p f t", f=11)
        sg_v = R3[:, 5632:5632 + 3 * TT].rearrange("p (r t) -> p r t", r=3)
        cg_v = R3[:, 0:2 * TT].rearrange("p (r t) -> p r t", r=2)
        cb_v = R3[:, 1024:1024 + 2 * (T + HB)].rearrange("p (c t) -> p c t", c=2)
        s_v = R3[:, 3200:3200 + 2 * TT].rearrange("p (r t) -> p r t", r=2)
        u_v = R3[:, 0:2 * (T + HU)].rearrange("p (c t) -> p c t", c=2)
        r1_v = R3[:, 2080:2080 + (T + HU)]
        r2_v = R3[:, 3120:3120 + (T + HU)]
        c16_v = R3[:, 4160:4176]
        p_v = R3[:, 4224:4224 + T].bitcast(BF16).rearrange("p (c t) -> p c t", c=2)
        tg_v = R3[:, 0:3 * 2 * TT].rearrange("p (i r t) -> p i r t", i=3, r=2)
        HKEYS = [("h", c_, t_) for c_ in range(KC) for t_ in range(NT)]
        stage_parts = [
            ("yo_a", a_act, "a_act", 0, 4),
            ("yo_b", sbb, "sbb", 4, 2),
            ("yo_c", pp, "pp", 6, 2),
        ]

        def stage_chunk(c):
            for (_, buf, nm, c0, n) in stage_parts:
                if c0 <= c < c0 + n:
                    return buf[:, c - c0, :].bitcast(F32), [(nm, c - c0, t_) for t_ in range(NT)]

        ident_v = consts[:, 0:128]
        corr_v = consts[:, 128:160].rearrange("p (c t) -> p c t", c=2)

        state = {"bank": 0, "sq": 0, "sg": 0, "tg": 0, "tga": 0}

        def new_bank():
            b = state["bank"]
            state["bank"] = (b + 1) % 8
            return b

        def rot(name, n):
            v = state[name]
            state[name] = (v + 1) % n
            return v

        total_units = n_tiles * NU
        wstate = {"issued": 0}

        def issue_unit():
            n = wstate["issued"]
            if n >= total_units:
                return
            wstate["issued"] = n + 1
            slot = n % NS
            u = n % NU
            S.op("pool", lambda e, slot=slot, u=u: e.dma_start(out=ring[:, slot, :], in_=wstream[u]),
                 writes=[("ring", slot)], dma_sem="ring%d" % slot)

        cursor = {"n": 0}

        def next_unit(expect_kind):
            n = cursor["n"]
            cursor["n"] = n + 1
            assert unit_list[n % NU][0] == expect_kind, (unit_list[n % NU], expect_kind)
            return n % NS

        def release_unit():
            issue_unit()

        S.op("sp", lambda e: e.dma_start(out=params[:, :], in_=params_d), writes=["params"], dma_sem="params")
        S.op("sp", lambda e: e.dma_start(out=consts[:, :], in_=consts_d), writes=["consts"], dma_sem="consts")
        S.op("dve", lambda e: e.memset(ones[:, :], 1.0), writes=["ones"])
        S.op("dve", lambda e: e.memset(ones_bf[:, :], 1.0), writes=["ones_bf"])
        S.op("dve", lambda e: e.memset(epsc[:, :], EPS), writes=["epsc"])
        for _ in range(NS):
            issue_unit()

        def load_x_tt(tile, tt):
            S.op("sp", lambda e, tile=tile, tt=tt: e.dma_start(
                out=x[:, :, tt * TT:(tt + 1) * TT], in_=xT[tile, :, :, tt * TT:(tt + 1) * TT]),
                writes=[("x", c, tt) for c in range(KC)], dma_sem="x%d" % tt)

        def norm_sq(tt):
            for c in range(KC):
                S.op("act", lambda e, c=c, tt=tt: e.activation(
                    out=sq[:, c, :], in_=x[:, c, tt * TT:(tt + 1) * TT], func=AF.Square),
                    reads=[("x", c, tt)], writes=[("sq", c)])

        def norm_fin(tt):
            b = new_bank()
            for c in range(KC):
                S.op("pe", lambda e, c=c, b=b: e.matmul(
                    ps[:, b, :], ones_bf[:, :], sq[:, c, :], start=(c == 0), stop=(c == KC - 1)),
                    reads=[("sq", c), "ones_bf"], writes=[("ps", b)])
            S.op("act", lambda e, b=b, tt=tt: e.activation(
                out=rs[:, tt, :], in_=ps[:, b, :], func=AF.Ln, bias=epsc[:, 0:1], scale=1.0 / D_MODEL),
                reads=[("ps", b), "epsc"], writes=[("rs", tt)])
            S.op("act", lambda e, tt=tt: e.activation(
                out=rs[:, tt, :], in_=rs[:, tt, :], func=AF.Exp, scale=-0.5),
                reads=[("rs", tt)], writes=[("rs", tt)])

        def norm_stats(tt):
            norm_sq(tt)
            norm_fin(tt)

        def norm_apply(tt, gcol):
            for c in range(KC):
                S.op("dve", lambda e, c=c, tt=tt, gcol=gcol: e.scalar_tensor_tensor(
                    out=h[:, c, tt * TT:(tt + 1) * TT], in0=x[:, c, tt * TT:(tt + 1) * TT],
                    scalar=params[:, gcol + c:gcol + c + 1], in1=rs[:, tt, :], op0=ALU.mult, op1=ALU.mult),
                    reads=[("x", c, tt), ("rs", tt), "params"], writes=[("h", c, tt)])

        def final_apply(tile, tt, gcol):
            for c in range(KC):
                sv, keys = stage_chunk(c)
                S.op("dve", lambda e, c=c, tt=tt, gcol=gcol, sv=sv: e.scalar_tensor_tensor(
                    out=sv, in0=x[:, c, tt * TT:(tt + 1) * TT],
                    scalar=params[:, gcol + c:gcol + c + 1], in1=rs[:, tt, :], op0=ALU.mult, op1=ALU.mult),
                    reads=[("x", c, tt), ("rs", tt), "params"], writes=keys)
            for (sem, buf, nm, c0, n) in stage_parts:
                src = buf[:, :, :].rearrange("p c t -> p (c t)").bitcast(F32).rearrange("p (c t) -> p c t", c=n)
                S.op("sp", lambda e, tile=tile, tt=tt, src=src, c0=c0, n=n: e.dma_start(
                    out=yT[tile, :, c0:c0 + n, tt * TT:(tt + 1) * TT], in_=src),
                    reads=[(nm, c_, t_) for c_ in range(n) for t_ in range(NT)], dma_sem=sem)
            if tile + 1 < n_tiles:
                load_x_tt(tile + 1, tt)

        def ffn(l, which, after_tt, first_block=0, mid_hook=None):
            for g, fs in enumerate(FFN_GROUPS):
                nfb = first_block if g == 0 else 0
                sched = []
                if nfb:
                    fslots = [next_unit("gu") for _ in range(nfb)]
                    for tt in range(NT):
                        for fi in range(nfb):
                            sched.append((fi, fslots[fi], tt, tt == NT - 1))
                for fi in range(nfb, len(fs)):
                    slot = None
                    for tt in range(NT):
                        sched.append((fi, slot, tt, tt == NT - 1))
                hook = [mid_hook if g == 0 else None]
                for (fi, slot, tt, rel) in sched:
                    if tt == 1 and hook[0] is not None:
                        hook[0]()
                        hook[0] = None
                    if slot is None:
                        if tt == 0:
                            cur_slot = next_unit("gu")
                        slot = cur_slot
                    for _once in (0,):
                        bg = new_bank()
                        for kc in range(KC):
                            S.op("pe", lambda e, bg=bg, slot=slot, kc=kc, tt=tt: e.matmul(
                                ps[:, bg, :], ring[:, slot, kc * 128:(kc + 1) * 128],
                                h[:, kc, tt * TT:(tt + 1) * TT], start=(kc == 0), stop=(kc == KC - 1)),
                                reads=[("ring", slot), ("h", kc, tt)], writes=[("ps", bg)])
                        bu = new_bank()
                        for kc in range(KC):
                            S.op("pe", lambda e, bu=bu, slot=slot, kc=kc, tt=tt: e.matmul(
                                ps[:, bu, :], ring[:, slot, 1024 + kc * 128:1024 + (kc + 1) * 128],
                                h[:, kc, tt * TT:(tt + 1) * TT], start=(kc == 0), stop=(kc == KC - 1)),
                                reads=[("ring", slot), ("h", kc, tt)], writes=[("ps", bu)])
                        k = rot("sg", 3)
                        S.op("act", lambda e, bg=bg, k=k: e.activation(
                            out=sg_v[:, k, :], in_=ps[:, bg, :], func=AF.Silu),
                            reads=[("ps", bg)], writes=[("sg", k)], regions=[("R3", "ffn")])
                        S.op("dve", lambda e, bu=bu, k=k, fi=fi, tt=tt: e.tensor_tensor(
                            out=act_v[:, fi, tt * TT:(tt + 1) * TT], in0=sg_v[:, k, :], in1=ps[:, bu, :],
                            op=ALU.mult),
                            reads=[("ps", bu), ("sg", k)], writes=[("act", fi, tt)], regions=[("R3", "ffn")])
                    if rel:
                        release_unit()
                nwd = (len(fs) + 1) // 2
                wslots = [next_unit("wd") for _ in range(nwd)]
                last = (g == len(FFN_GROUPS) - 1)
                for tt in range(NT):
                    for d in range(KC):
                        b = new_bank()
                        for fi in range(len(fs)):
                            ws_ = wslots[fi // 2]
                            off = (fi % 2) * 1024 + d * 128
                            S.op("pe", lambda e, b=b, ws_=ws_, off=off, fi=fi, tt=tt, n=len(fs): e.matmul(
                                ps[:, b, :], ring[:, ws_, off:off + 128], act_v[:, fi, tt * TT:(tt + 1) * TT],
                                start=(fi == 0), stop=(fi == n - 1)),
                                reads=[("ring", ws_), ("act", fi, tt)], writes=[("ps", b)],
                                regions=[("R3", "ffn")])
                        S.op("dve", lambda e, b=b, d=d, tt=tt: e.scalar_tensor_tensor(
                            out=x[:, d, tt * TT:(tt + 1) * TT], in0=ps[:, b, :], scalar=0.5,
                            in1=x[:, d, tt * TT:(tt + 1) * TT], op0=ALU.mult, op1=ALU.add),
                            reads=[("ps", b), ("x", d, tt)], writes=[("x", d, tt)])
                        if last and tt == 1 and d == 2:
                            after_tt.fin(0)
                    if last:
                        after_tt.sq(tt)
                if last:
                    after_tt.fin(NT - 1)
                for _ in range(nwd):
                    release_unit()

        def win_chunk_mm(slot, half_idx, tt):
            b = new_bank()
            for kc in range(KC):
                S.op("pe", lambda e, b=b, slot=slot, kc=kc, tt=tt, o=half_idx * 1024: e.matmul(
                    ps[:, b, :], ring[:, slot, o + kc * 128:o + (kc + 1) * 128],
                    h[:, kc, tt * TT:(tt + 1) * TT], start=(kc == 0), stop=(kc == KC - 1)),
                    reads=[("ring", slot), ("h", kc, tt)], writes=[("ps", b)])
            return b

        def mixer(l, half, after_tt):
            pb = l * PL
            RA = [("R1", "a")]
            RCO = [("R2", "co")]
            for j in range(4):
                if half == 0:
                    S.op("dve", lambda e, j=j: e.memset(a_v[:, j, 0:HA], 0.0),
                         writes=[("a_halo", j)], regions=RA)
                else:
                    S.op("dve", lambda e, j=j, l=l: e.tensor_copy(a_v[:, j, 0:HA].bitcast(F32R), halo_a[:, l, j, :]),
                         reads=[("halo_a", l, j)], writes=[("a_halo", j)], regions=RA)

            def av(j):
                slot = next_unit("win")
                for tt in range(NT):
                    b1 = win_chunk_mm(slot, 0, tt)
                    k = rot("tga", 3)
                    S.op("act", lambda e, b1=b1, k=k: e.activation(
                        out=tga[:, k, :], in_=ps[:, b1, :], func=AF.Tanh, scale=0.5),
                        reads=[("ps", b1)], writes=[("tga", k)])
                    b2 = win_chunk_mm(slot, 1, tt)
                    S.op("dve", lambda e, b2=b2, k=k, j=j, tt=tt: e.scalar_tensor_tensor(
                        out=a_v[:, j, HA + tt * TT:HA + (tt + 1) * TT].bitcast(F32R), in0=tga[:, k, :], scalar=1.0,
                        in1=ps[:, b2, :], op0=ALU.add, op1=ALU.mult),
                        reads=[("ps", b2), ("tga", k)], writes=[("a", j, tt)], regions=RA)
                release_unit()
                if half == 0:
                    S.op("dve", lambda e, j=j, l=l: e.tensor_copy(halo_a[:, l, j, :], a_v[:, j, T:T + HA]),
                         reads=[("a", j, NT - 1)], writes=[("halo_a", l, j)], regions=RA)

            DH = 16

            def diag_build(j):
                for (eng, k0, k1, hk) in (("dve", 0, DH, 0), ("pool", DH, CW, 1)):
                    cwv = params[:, pb + P_CW + j * CW + k0: pb + P_CW + j * CW + k1]
                    S.op(eng, lambda e, cwv=cwv, k0=k0, k1=k1: e.tensor_tensor(
                        out=diag[:, k0:k1, :].bitcast(F32R),
                        in0=ident_v.unsqueeze(1).to_broadcast([128, k1 - k0, 128]),
                        in1=cwv.unsqueeze(2).to_broadcast([128, k1 - k0, 128]), op=ALU.mult),
                        reads=["consts", "params"], writes=[("diag", hk)])

            def conv(j):
                banks = [new_bank() for _ in range(NT)]
                for k in range(CW):
                    for tt in range(NT):
                        rd = [("a", j, tt), ("diag", 0 if k < DH else 1)]
                        if tt == 0:
                            rd.append(("a_halo", j))
                        else:
                            rd.append(("a", j, tt - 1))
                        S.op("pe", lambda e, k=k, tt=tt, j=j, b=banks[tt]: e.matmul(
                            ps[:, b, :], diag[:, k, :].bitcast(F32R),
                            a_v[:, j, tt * TT + k:tt * TT + k + TT].bitcast(F32R),
                            start=(k == 0), stop=(k == CW - 1)),
                            reads=rd, writes=[("ps", banks[tt])], regions=RA)
                for tt in range(NT):
                    S.op("act", lambda e, tt=tt, j=j, b=banks[tt]: e.activation(
                        out=co_v[:, j, tt * TT:(tt + 1) * TT].bitcast(F32R), in_=ps[:, b, :], func=AF.Identity,
                        bias=params[:, pb + P_CB + j:pb + P_CB + j + 1], scale=0.5),
                        reads=[("ps", banks[tt]), "params"], writes=[("co", j, tt)], regions=RCO)

            def ln_sq(tt):
                for j in range(4):
                    k = tt * 4 + j
                    S.op("act", lambda e, j=j, k=k, tt=tt: e.activation(
                        out=sq[:, k, :], in_=co_v[:, j, tt * TT:(tt + 1) * TT], func=AF.Square),
                        reads=[("co", j, tt)], writes=[("sq", k)], regions=RCO)

            def ln_fin(tt):
                b1 = new_bank()
                b2 = new_bank()
                for j in range(4):
                    k = tt * 4 + j
                    S.op("pe", lambda e, j=j, tt=tt, b1=b1: e.matmul(
                        ps[:, b1, :], ones[:, :].bitcast(F32R), co_v[:, j, tt * TT:(tt + 1) * TT].bitcast(F32R),
                        start=(j == 0), stop=(j == 3)),
                        reads=[("co", j, tt), "ones"], writes=[("ps", b1)], regions=RCO)
                    S.op("pe", lambda e, j=j, k=k, b2=b2: e.matmul(
                        ps[:, b2, :], ones_bf[:, :], sq[:, k, :],
                        start=(j == 0), stop=(j == 3)),
                        reads=[("sq", k), "ones_bf"], writes=[("ps", b2)])
                S.op("dve", lambda e, tt=tt, b1=b1: e.tensor_scalar(
                    lnm[:, tt, :], ps[:, b1, :], 1.0 / 512.0, None, op0=ALU.mult),
                    reads=[("ps", b1)], writes=[("lnm", tt)])
                S.op("dve", lambda e, tt=tt: e.tensor_tensor(
                    out=lnq[:, :], in0=lnm[:, tt, :], in1=lnm[:, tt, :], op=ALU.mult),
                    reads=[("lnm", tt)], writes=["lnq"])
                S.op("dve", lambda e, tt=tt, b2=b2: e.scalar_tensor_tensor(
                    out=lnv[:, tt, :], in0=ps[:, b2, :], scalar=1.0 / 512.0, in1=lnq[:, :],
                    op0=ALU.mult, op1=ALU.subtract),
                    reads=[("ps", b2), "lnq"], writes=[("lnv", tt)])
                S.op("act", lambda e, tt=tt: e.activation(
                    out=lnv[:, tt, :], in_=lnv[:, tt, :], func=AF.Ln, bias=epsc[:, 0:1], scale=1.0),
                    reads=[("lnv", tt), "epsc"], writes=[("lnv", tt)])
                S.op("act", lambda e, tt=tt: e.activation(
                    out=lnv[:, tt, :], in_=lnv[:, tt, :], func=AF.Exp, scale=-0.5),
                    reads=[("lnv", tt)], writes=[("lnv", tt)])

            def ln_apply(tt):
                for j in range(4):
                    S.op("dve", lambda e, j=j, tt=tt: e.tensor_tensor(
                        out=co_v[:, j, tt * TT:(tt + 1) * TT].bitcast(F32R), in0=co_v[:, j, tt * TT:(tt + 1) * TT],
                        in1=lnm[:, tt, :], op=ALU.subtract),
                        reads=[("co", j, tt), ("lnm", tt)], writes=[("co", j, tt)], regions=RCO)
                    S.op("dve", lambda e, j=j, tt=tt: e.tensor_tensor(
                        out=co_v[:, j, tt * TT:(tt + 1) * TT].bitcast(F32R), in0=co_v[:, j, tt * TT:(tt + 1) * TT],
                        in1=lnv[:, tt, :], op=ALU.mult),
                        reads=[("co", j, tt), ("lnv", tt)], writes=[("co", j, tt)], regions=RCO)
                    S.op("act", lambda e, j=j, tt=tt: e.activation(
                        out=a_act[:, j, tt * TT:(tt + 1) * TT], in_=co_v[:, j, tt * TT:(tt + 1) * TT],
                        func=AF.Silu, bias=params[:, pb + P_LB + j:pb + P_LB + j + 1],
                        scale=params[:, pb + P_LG + j:pb + P_LG + j + 1]),
                        reads=[("co", j, tt), "params"], writes=[("a_act", j, tt)], regions=RCO)

            RB = [("R3", "B")]

            def b_cgate(c, slot, hi, tts=tuple(range(NT))):
                for tt in tts:
                    b = win_chunk_mm(slot, hi, tt)
                    S.op("act", lambda e, b=b, c=c, tt=tt: e.activation(
                        out=cg_v[:, tt, :], in_=ps[:, b, :], func=AF.Copy),
                        reads=[("ps", b)], writes=[("cg", tt)], regions=RB)

            def b_bx(c, slot, hi, tts=tuple(range(NT))):
                if 0 not in tts:
                    pass
                elif half == 0:
                    S.op("dve", lambda e, c=c: e.memset(cb_v[:, c, 0:HB], 0.0),
                         writes=[("cb_halo", c)], regions=RB)
                else:
                    S.op("dve", lambda e, c=c, l=l: e.tensor_copy(cb_v[:, c, 0:HB], halo_cb[:, l, c, :]),
                         reads=[("halo_cb", l, c)], writes=[("cb_halo", c)], regions=RB)
                for tt in tts:
                    b = win_chunk_mm(slot, hi, tt)
                    S.op("dve", lambda e, b=b, c=c, tt=tt: e.tensor_tensor(
                        out=cb_v[:, c, HB + tt * TT:HB + (tt + 1) * TT], in0=cg_v[:, tt, :], in1=ps[:, b, :],
                        op=ALU.mult),
                        reads=[("ps", b), ("cg", tt)], writes=[("cb", c, tt)], regions=RB)
                if half == 0 and (NT - 1) in tts:
                    S.op("dve", lambda e, c=c, l=l: e.tensor_copy(halo_cb[:, l, c, :], cb_v[:, c, T:T + HB]),
                         reads=[("cb", c, NT - 1)], writes=[("halo_cb", l, c)], regions=RB)

            def b_bgate(c, slot, hi, tts=tuple(range(NT))):
                for tt in tts:
                    rd = [("cb", c, tt), "params"] + ([("cb_halo", c)] if tt == 0 else [("cb", c, tt - 1)])
                    for k in range(3):
                        wcol = pb + P_SW + c * 3 + k
                        if k == 0:
                            S.op("dve", lambda e, c=c, tt=tt, wcol=wcol: e.tensor_scalar(
                                s_v[:, tt, :], cb_v[:, c, tt * TT:tt * TT + TT], params[:, wcol:wcol + 1], None,
                                op0=ALU.mult),
                                reads=rd, writes=[("s", tt)], regions=RB)
                        else:
                            S.op("dve", lambda e, c=c, tt=tt, k=k, wcol=wcol: e.scalar_tensor_tensor(
                                out=s_v[:, tt, :], in0=cb_v[:, c, tt * TT + k:tt * TT + k + TT],
                                scalar=params[:, wcol:wcol + 1], in1=s_v[:, tt, :], op0=ALU.mult, op1=ALU.add),
                                reads=rd + [("s", tt)], writes=[("s", tt)], regions=RB)
                    b = win_chunk_mm(slot, hi, tt)
                    S.op("dve", lambda e, b=b, c=c, tt=tt: e.tensor_tensor(
                        out=sbb[:, c, tt * TT:(tt + 1) * TT], in0=s_v[:, tt, :], in1=ps[:, b, :], op=ALU.mult),
                        reads=[("ps", b), ("s", tt)], writes=[("sbb", c, tt)], regions=RB)

            RC = [("R3", "C")]

            def c_pin(c, slot, hi):
                if half == 0:
                    S.op("dve", lambda e, c=c: e.memset(u_v[:, c, 0:HU], 0.0), writes=[("u_halo", c)], regions=RC)
                else:
                    S.op("dve", lambda e, c=c, l=l: e.tensor_copy(u_v[:, c, 0:HU], halo_u[:, l, c, :]),
                         reads=[("halo_u", l, c)], writes=[("u_halo", c)], regions=RC)
                for tt in range(NT):
                    b = win_chunk_mm(slot, hi, tt)
                    S.op("act", lambda e, b=b, c=c, tt=tt: e.activation(
                        out=u_v[:, c, HU + tt * TT:HU + (tt + 1) * TT], in_=ps[:, b, :], func=AF.Copy),
                        reads=[("ps", b)], writes=[("u", c, tt)], regions=RC)
                if half == 0:
                    S.op("dve", lambda e, c=c, l=l: e.tensor_copy(halo_u[:, l, c, :], u_v[:, c, T:T + HU]),
                         reads=[("u", c, NT - 1)], writes=[("halo_u", l, c)], regions=RC)

            def c_pool(c):
                W = T + HU
                uall = [("u", c, tt) for tt in range(NT)] + [("u_halo", c)]
                uu = u_v[:, c, :]
                S.op("dve", lambda e, uu=uu: e.tensor_tensor(
                    out=r1_v[:, 1:W], in0=uu[:, 1:W], in1=uu[:, 0:W - 1], op=ALU.add),
                    reads=uall, writes=["r1"], regions=RC)
                if c == 0:
                    S.op("dve", lambda e: e.tensor_tensor(
                        out=r2_v[64:128, 3:W], in0=r1_v[64:128, 3:W], in1=r1_v[64:128, 1:W - 2], op=ALU.add),
                        reads=["r1"], writes=["r2"], regions=RC)
                    lo, hi_, wl, wh = r1_v, r2_v, 2, 4
                else:
                    S.op("dve", lambda e: e.tensor_tensor(
                        out=r2_v[:, 3:W], in0=r1_v[:, 3:W], in1=r1_v[:, 1:W - 2], op=ALU.add),
                        reads=["r1"], writes=["r2"], regions=RC)
                    S.op("dve", lambda e: e.tensor_tensor(
                        out=r1_v[:, 7:W], in0=r2_v[:, 7:W], in1=r2_v[:, 3:W - 4], op=ALU.add),
                        reads=["r2"], writes=["r1"], regions=RC)
                    S.op("dve", lambda e: e.tensor_tensor(
                        out=r2_v[64:128, 15:W], in0=r1_v[64:128, 15:W], in1=r1_v[64:128, 7:W - 8], op=ALU.add),
                        reads=["r1"], writes=["r2"], regions=RC)
                    lo, hi_, wl, wh = r1_v, r2_v, 8, 16
                for (rr, p0, p1, w) in ((lo, 0, 64, wl), (hi_, 64, 128, wh)):
                    S.op("dve", lambda e, rr=rr, p0=p0, p1=p1, w=w, c=c: e.scalar_tensor_tensor(
                        out=p_v[p0:p1, c, :], in0=rr[p0:p1, HU:W], scalar=1.0 / w, in1=u_v[p0:p1, c, HU:W],
                        op0=ALU.mult, op1=ALU.subtract),
                        reads=["r1", "r2"] + uall, writes=[("p", c, 0), ("p", c, 1)], regions=RC)
                    if half == 0:
                        S.op("dve", lambda e, rr=rr, p0=p0, p1=p1, c=c: e.tensor_tensor(
                            out=c16_v[p0:p1, :], in0=rr[p0:p1, HU:HU + 16], in1=corr_v[p0:p1, c, :], op=ALU.mult),
                            reads=["r1", "r2", "consts"], writes=["c16"], regions=RC)
                        S.op("dve", lambda e, p0=p0, p1=p1, c=c: e.tensor_tensor(
                            out=p_v[p0:p1, c, 0:16], in0=c16_v[p0:p1, :], in1=u_v[p0:p1, c, HU:HU + 16],
                            op=ALU.subtract),
                            reads=["c16"] + uall, writes=[("p", c, 0)], regions=RC)

            def c_pool_mm(c, pslot):
                for tt in range(NT):
                    b = new_bank()
                    S.op("pe", lambda e, b=b, c=c, tt=tt, pslot=pslot: e.matmul(
                        ps[:, b, :], ring[:, pslot, c * 128:(c + 1) * 128], p_v[:, c, tt * TT:(tt + 1) * TT],
                        start=True, stop=True),
                        reads=[("ring", pslot), ("p", c, tt)], writes=[("ps", b)], regions=RC)
                    S.op("act", lambda e, b=b, c=c, tt=tt: e.activation(
                        out=pp[:, c, tt * TT:(tt + 1) * TT], in_=ps[:, b, :], func=AF.Copy,
                        scale=params[:, pb + P_PS + c:pb + P_PS + c + 1]),
                        reads=[("ps", b), "params"], writes=[("pp", c, tt)])

            s0 = next_unit("win")
            s1 = next_unit("win")
            for tts in ((0,), (1,)):
                b_cgate(0, s0, 0, tts)
                b_bx(0, s0, 1, tts)
                b_bgate(0, s1, 0, tts)
                b_cgate(1, s1, 1, tts)
            release_unit()
            release_unit()
            s2 = next_unit("win")
            b_bx(1, s2, 0)
            b_bgate(1, s2, 1)
            release_unit()
            s3 = next_unit("win")
            c_pin(0, s3, 0)
            c_pin(1, s3, 1)
            release_unit()
            diag_build(0)
            av(0)
            c_pool(0)
            av(1)
            conv(0)
            diag_build(1)
            c_pool(1)
            av(2)
            conv(1)
            diag_build(2)
            av(3)
            conv(2)
            diag_build(3)
            pslot = next_unit("poolw")
            c_pool_mm(0, pslot)
            c_pool_mm(1, pslot)
            release_unit()
            conv(3)
            for tt in range(NT):
                ln_sq(tt)
            ln_pending = [True]

            if debug and l == 0 and half == 0:
                S.op("sp", lambda e: e.dma_start(out=dbg["a"], in_=a_v[:, :, :]),
                     reads=[("a", j_, t_) for j_ in range(4) for t_ in range(NT)], regions=RA, dma_sem="dbg")
                S.op("sp", lambda e: e.dma_start(out=dbg["co"], in_=co_v[:, :, :]),
                     reads=[("co", j_, t_) for j_ in range(4) for t_ in range(NT)]
                     + [("a_act", j_, t_) for j_ in range(4) for t_ in range(NT)], regions=RCO, dma_sem="dbg")
                S.op("sp", lambda e: e.dma_start(out=dbg["aact"], in_=a_act[:, :, :]),
                     reads=[("a_act", j_, t_) for j_ in range(4) for t_ in range(NT)], dma_sem="dbg")
                S.op("sp", lambda e: e.dma_start(out=dbg["sbb"], in_=sbb[:, :, :]),
                     reads=[("sbb", j_, t_) for j_ in range(2) for t_ in range(NT)], dma_sem="dbg")
                S.op("sp", lambda e: e.dma_start(out=dbg["pp"], in_=pp[:, :, :]),
                     reads=[("pp", j_, t_) for j_ in range(2) for t_ in range(NT)], dma_sem="dbg")
                S.op("sp", lambda e: e.dma_start(out=dbg["h"], in_=h[:, :, :]),
                     reads=HKEYS, dma_sem="dbg")
            RP = [("R3", "P2")]
            RM = RP
            RT = RP
            for j in range(KC):
                sa = next_unit("p2a")
                sb_ = next_unit("p2b")
                rr = []
                for tt in range(NT):
                    r = rot("tg", 2)
                    rr.append(r)
                    gbanks = []
                    for i, (slot, off) in enumerate(((sa, 0), (sa, 1024), (sb_, 0))):
                        b = new_bank()
                        gbanks.append(b)
                        for kc in range(KC):
                            S.op("pe", lambda e, b=b, slot=slot, off=off, kc=kc, tt=tt: e.matmul(
                                ps[:, b, :], ring[:, slot, off + kc * 128:off + (kc + 1) * 128],
                                h[:, kc, tt * TT:(tt + 1) * TT], start=(kc == 0), stop=(kc == KC - 1)),
                                reads=[("ring", slot), ("h", kc, tt)], writes=[("ps", b)])
                    if ln_pending[0]:
                        ln_fin(tt)
                    for i, b in enumerate(gbanks):
                        S.op("act", lambda e, b=b, i=i, r=r: e.activation(
                            out=tg_v[:, i, r, :], in_=ps[:, b, :], func=AF.Tanh, scale=0.5),
                            reads=[("ps", b)], writes=[("tg", i, r)], regions=RT)
                    if ln_pending[0]:
                        ln_apply(tt)
                ln_pending[0] = False
                for tt in range(NT):
                    r = rr[tt]
                    ybanks = [None, None, None]
                    for (yi, src, nk, off, nm) in ((1, sbb, 2, 1536, "sbb"), (2, pp, 2, 1792, "pp"), (0, a_act, 4, 1024, "a_act")):
                        b = new_bank()
                        for kc in range(nk):
                            S.op("pe", lambda e, b=b, src=src, kc=kc, nk=nk, off=off, tt=tt, sb_=sb_: e.matmul(
                                ps[:, b, :], ring[:, sb_, off + kc * 128:off + (kc + 1) * 128],
                                src[:, kc, tt * TT:(tt + 1) * TT], start=(kc == 0), stop=(kc == nk - 1)),
                                reads=[("ring", sb_), (nm, kc, tt)], writes=[("ps", b)])
                        ybanks[yi] = b
                    for i in range(3):
                        S.op("dve", lambda e, i=i, r=r, b=ybanks[i]: e.scalar_tensor_tensor(
                            out=tg_v[:, i, r, :], in0=tg_v[:, i, r, :], scalar=1.0, in1=ps[:, b, :],
                            op0=ALU.add, op1=ALU.mult),
                            reads=[("ps", ybanks[i]), ("tg", i, r)], writes=[("tg", i, r)], regions=RP)
                    S.op("dve", lambda e, r=r: e.tensor_tensor(
                        out=tg_v[:, 0, r, :], in0=tg_v[:, 0, r, :], in1=tg_v[:, 1, r, :], op=ALU.add),
                        reads=[("tg", 0, r), ("tg", 1, r)], writes=[("tg", 0, r)], regions=RP)
                    S.op("dve", lambda e, r=r, j=j, tt=tt: e.tensor_tensor(
                        out=mg_v[:, j, tt * TT:(tt + 1) * TT], in0=tg_v[:, 0, r, :], in1=tg_v[:, 2, r, :],
                        op=ALU.add),
                        reads=[("tg", 0, r), ("tg", 2, r)], writes=[("mg", j, tt)], regions=RP)
                release_unit()
                release_unit()
            if debug and l == 0 and half == 0:
                S.op("sp", lambda e: e.dma_start(out=dbg["mg"], in_=mg_v[:, :, :]),
                     reads=[("mg", j_, t_) for j_ in range(KC) for t_ in range(NT)], regions=RP, dma_sem="dbg")
            wos = [next_unit("wo") for _ in range(4)]
            for tt in range(NT):
                for d in range(KC):
                    b = new_bank()
                    for kc in range(KC):
                        off = (d % 2) * 1024 + kc * 128
                        S.op("pe", lambda e, b=b, sl=wos[d // 2], off=off, kc=kc, tt=tt: e.matmul(
                            ps[:, b, :], ring[:, sl, off:off + 128], mg_v[:, kc, tt * TT:(tt + 1) * TT],
                            start=(kc == 0), stop=(kc == KC - 1)),
                            reads=[("ring", wos[d // 2]), ("mg", kc, tt)], writes=[("ps", b)], regions=RM)
                    S.op("dve", lambda e, b=b, d=d, tt=tt: e.scalar_tensor_tensor(
                        out=x[:, d, tt * TT:(tt + 1) * TT], in0=ps[:, b, :], scalar=0.5,
                        in1=x[:, d, tt * TT:(tt + 1) * TT], op0=ALU.mult, op1=ALU.add),
                        reads=[("ps", b), ("x", d, tt)], writes=[("x", d, tt)])
                    if tt == 1 and d == 2:
                        after_tt.fin(0)
                after_tt.sq(tt)
            after_tt.fin(NT - 1)
            for _ in range(4):
                release_unit()

        for tile in range(n_tiles):
            half = tile % HALVES
            if tile == 0:
                for tt in range(NT):
                    load_x_tt(tile, tt)
            first = [st for st in ALL_STAGES if st in stages]
            g0 = {"ffn1": P_N1, "mix": P_NM, "ffn2": P_N2}[first[0]] if first else P_N1
            norm_stats(0)
            norm_apply(0, 0 * PL + g0)

            def tile_start_tt1(g0=g0):
                norm_stats(1)
                norm_apply(1, 0 * PL + g0)
            ts_hook = [tile_start_tt1]
            for l in range(depth):
                pb = l * PL

                class After:
                    def __init__(self, fin_fn, defer0=True):
                        self.fin_fn = fin_fn
                        self.defer0 = defer0
                        self.done = set()

                    def sq(self, tt):
                        norm_sq(tt)
                        if tt == 0 and not self.defer0:
                            self.fin(0)

                    def fin(self, tt):
                        if tt in self.done:
                            return
                        self.done.add(tt)
                        norm_fin(tt)
                        self.fin_fn(tt)

                def fin_ffn1(tt, pb=pb):
                    norm_apply(tt, pb + P_NM)

                def fin_mix(tt, pb=pb):
                    norm_apply(tt, pb + P_N2)

                def fin_ffn2(tt, l=l, tile=tile, g0=g0):
                    if l + 1 < depth:
                        norm_apply(tt, (l + 1) * PL + g0)
                    else:
                        final_apply(tile, tt, depth * PL)
                after_ffn1, after_mix = After(fin_ffn1), After(fin_mix)
                after_ffn2 = After(fin_ffn2, defer0=(l + 1 < depth))

                seq = []
                if "ffn1" in stages:
                    seq.append("ffn1")
                if "mix" in stages:
                    seq.append("mix")
                if "ffn2" in stages:
                    seq.append("ffn2")
                for si, st in enumerate(seq):
                    if si + 1 < len(seq):
                        nxt = {"mix": after_ffn1, "ffn2": after_mix}[seq[si + 1]]
                    else:
                        nxt = after_ffn2
                    if l == 0 and si == 0 and st != "ffn1" and ts_hook[0] is not None:
                        ts_hook[0]()
                        ts_hook[0] = None
                    if st == "ffn1":
                        ts = (l == 0 and si == 0)
                        mh = ts_hook[0] if ts else None
                        if ts:
                            ts_hook[0] = None
                        ffn(l, 1, nxt, first_block=(7 if ts else 4), mid_hook=mh)
                    elif st == "mix":
                        mixer(l, half, nxt)
                    else:
                        ffn(l, 2, nxt, first_block=4)
                if not seq:
                    if ts_hook[0] is not None:
                        ts_hook[0]()
                        ts_hook[0] = None
                    for tt in range(NT):
                        after_ffn2.sq(tt)
                        after_ffn2.fin(tt)
        assert cursor["n"] == total_units, (cursor["n"], total_units)
        S.finalize()

        dma_names = sorted(S.dma_counts.keys())
        sems = {}
        for e_ in ("pe", "act", "dve", "pool"):
            sems[e_] = es.enter_context(nc.semaphore("s_" + e_))
        dsems = {}
        for nm in dma_names:
            dsems[nm] = es.enter_context(nc.semaphore("d_" + nm))
        block = es.enter_context(nc.Block())
        def emit_all():
            def run(engname):
                def body(e):
                    for o in S.q[engname]:
                        ws = []
                        for d in o.waits:
                            if d.dma_sem is not None:
                                ws.append((dsems[d.dma_sem], 16 * d.dma_val))
                            else:
                                ws.append((sems[d.eng], d.mval))
                        for (s_, v) in ws[1:]:
                            e.wait_ge(s_, v)
                        ins = o.fn(e)
                        if ws:
                            ins._wait_ge(ws[0][0], ws[0][1])
                        if o.dma_sem is not None:
                            ins.then_inc(dsems[o.dma_sem], 16)
                        elif o.milestone:
                            ins.then_inc(sems[engname], 1)
                    if engname == "sp":
                        for sem_ in ("yo_a", "yo_b", "yo_c"):
                            e.wait_ge(dsems[sem_], 16 * S.dma_counts[sem_])
                        if "dbg" in dsems:
                            e.wait_ge(dsems["dbg"], 16 * S.dma_counts["dbg"])
                return body
            block.tensor(run("pe"))
            block.scalar(run("act"))
            block.vector(run("dve"))
            block.gpsimd(run("pool"))
            block.sync(run("sp"))
        emit_all()
    return nc


def _to_tiles(xc):
    nseq = xc.shape[0]
    v = xc.reshape(nseq, HALVES, T, KC, 128)
    v = v.transpose(0, 1, 4, 3, 2)
    return np.ascontiguousarray(v).reshape(nseq * HALVES, 128, KC, T)


def _from_tiles(yt, nseq):
    v = yt.reshape(nseq, HALVES, 128, KC, T).transpose(0, 1, 4, 3, 2)
    return np.ascontiguousarray(v).reshape(nseq, SEQ, D_MODEL)


def kernel(**inputs):
    inp = {k: np.asarray(v) for k, v in inputs.items()}
    x = np.asarray(inp["x"], np.float32)
    seq_per_core = BATCH // N_CORES
    n_tiles = seq_per_core * HALVES
    ws = build_wstream(inp, DEPTH)
    params = build_params(inp, DEPTH)
    consts = build_consts()
    nc = build_program(n_tiles, DEPTH)
    in_maps = []
    for i in range(N_CORES):
        xt = _to_tiles(x[i * seq_per_core:(i + 1) * seq_per_core])
        in_maps.append({"xT": xt, "wstream": ws, "params": params, "consts": consts})
    res = run_bass_kernel_spmd(nc, in_maps, core_ids=list(range(N_CORES)))
    outs = [_from_tiles(np.asarray(r["yT"]), seq_per_core) for r in res.results]
    return np.concatenate(outs, axis=0).astype(np.float32)
```
